# Optimizing a Trainium2 kernel written in Bass

```python
import jax, jax.numpy as jnp
from jax import lax
import numpy as np

D_MODEL = 1024
BATCH = 8
SEQ = 4096
DEPTH = 4

N_EVEN = (DEPTH + 1) // 2
N_ODD = DEPTH // 2

A_WIDTH = D_MODEL // 2
A_HEADS = 4
A_HEAD_DIM = A_WIDTH // A_HEADS
MLSTM_CHUNK = 128
CONV_WIDTH = 5

B_WIDTH = D_MODEL - A_WIDTH
B_GROUPS = 4
B_GROUP_DIM = B_WIDTH // B_GROUPS
SGU_CHUNK = 128

AB_IN = 4 * A_WIDTH + 4 * A_HEADS + 2 * B_WIDTH

C_HEADS = 16
C_NOPE = 64
C_ROPE = 32
C_V = 64
Q_LORA = 384
KV_LORA = 256
C_IN = Q_LORA + KV_LORA + C_ROPE
ROPE_BASE = 10000.0
Q_BLOCK = 128

D_FF = 4 * D_MODEL

EPS = 1e-6

kernel_name = "hybrid_mlstm_sgu_mla_encoder"


def rms_norm(x, g):
    xf = x.astype(jnp.float32)
    y = xf * lax.rsqrt(jnp.mean(xf * xf, axis=-1, keepdims=True) + EPS)
    return (y * g.astype(jnp.float32)).astype(x.dtype)


def centred_dwconv(x, w):
    pad = (w.shape[0] - 1) // 2
    return lax.conv_general_dilated(
        x, w[:, None, :].astype(x.dtype), window_strides=(1,),
        padding=[(pad, pad)], dimension_numbers=("NWC", "WIO", "NWC"),
        feature_group_count=x.shape[-1])


def mlstm_scan(q, k, v, log_i, log_f):
    bsz, nh, seq, dh = q.shape
    L = MLSTM_CHUNK
    nc = seq // L

    def to_chunks(t):
        return jnp.moveaxis(t.reshape(bsz, nh, nc, L, *t.shape[3:]), 2, 0)

    lower = jnp.tril(jnp.ones((L, L), dtype=bool))

    def step(carry, xs):
        C, n, m = carry
        qc, kc, vc, li, lf = xs
        b = jnp.cumsum(lf, axis=-1)
        dmat = jnp.where(lower, b[..., :, None] - b[..., None, :] + li[..., None, :], -jnp.inf)
        m_t = jnp.maximum(b + m[..., None], jnp.max(dmat, axis=-1))
        inter = jnp.exp(b + m[..., None] - m_t)
        s = jnp.einsum("bhtd,bhsd->bhts", qc, kc) * jnp.exp(dmat - m_t[..., None])
        num = (jnp.einsum("bhts,bhse->bhte", s, vc)
               + inter[..., None] * jnp.einsum("bhtd,bhde->bhte", qc, C))
        den = jnp.sum(s, axis=-1) + inter * jnp.einsum("bhtd,bhd->bht", qc, n)
        h = num / jnp.maximum(jnp.abs(den), jnp.exp(-m_t))[..., None]
        g = b[..., -1]
        w_log = g[..., None] - b + li
        m_new = jnp.maximum(g + m, jnp.max(w_log, axis=-1))
        decay = jnp.exp(g + m - m_new)
        w = jnp.exp(w_log - m_new[..., None])
        C_new = decay[..., None, None] * C + jnp.einsum("bhs,bhsd,bhse->bhde", w, kc, vc)
        n_new = decay[..., None] * n + jnp.einsum("bhs,bhsd->bhd", w, kc)
        return (C_new, n_new, m_new), h

    init = (jnp.zeros((bsz, nh, dh, dh), jnp.float32),
            jnp.zeros((bsz, nh, dh), jnp.float32),
            jnp.zeros((bsz, nh), jnp.float32))
    _, h = lax.scan(step, init, (to_chunks(q), to_chunks(k), to_chunks(v),
                                 to_chunks(log_i), to_chunks(log_f)))
    return jnp.moveaxis(h, 0, 2).reshape(bsz, nh, seq, dh)


def ab_mixer(h, w_in, conv_w, gate_b, head_g, v_g, ws, bs, w_out):
    bsz, seq, _ = h.shape
    p = h @ w_in
    qk, va, oa, gates, uv = jnp.split(
        p, [2 * A_WIDTH, 3 * A_WIDTH, 4 * A_WIDTH, 4 * A_WIDTH + 4 * A_HEADS], axis=-1)

    qk = jax.nn.silu(centred_dwconv(qk, conv_w))

    def heads(t):
        return t.reshape(bsz, seq, A_HEADS, A_HEAD_DIM).transpose(0, 2, 1, 3).astype(jnp.float32)

    q = heads(qk[..., :A_WIDTH]) * (A_HEAD_DIM ** -0.5)
    k = heads(qk[..., A_WIDTH:])
    v = heads(va)
    g = (gates + gate_b).astype(jnp.float32).reshape(bsz, seq, 4, A_HEADS).transpose(2, 0, 3, 1)
    li_f, lf_f = g[0], jax.nn.log_sigmoid(g[1])
    li_b, lf_b = g[2], jax.nn.log_sigmoid(g[3])
    h_fwd = mlstm_scan(q, k, v, li_f, lf_f)
    fl = lambda t: jnp.flip(t, axis=2)
    h_bwd = fl(mlstm_scan(fl(q), fl(k), fl(v), jnp.flip(li_b, -1), jnp.flip(lf_b, -1)))
    ha = rms_norm(h_fwd + h_bwd, head_g.reshape(A_HEADS, 1, A_HEAD_DIM))
    ha = ha.transpose(0, 2, 1, 3).reshape(bsz, seq, A_WIDTH).astype(h.dtype) * jax.nn.sigmoid(oa)

    u, vb = jnp.split(jax.nn.gelu(uv), 2, axis=-1)
    vb = rms_norm(vb.reshape(bsz, seq, B_GROUPS, B_GROUP_DIM), v_g.reshape(B_GROUPS, B_GROUP_DIM))
    vb = vb.reshape(bsz, seq // SGU_CHUNK, SGU_CHUNK, B_GROUPS, B_GROUP_DIM)
    sp = jnp.einsum("gts,bnsgc->bntgc", ws, vb) + bs.T[:, :, None]
    hb = u * sp.reshape(bsz, seq, B_WIDTH)

    return jnp.concatenate([ha, hb], axis=-1) @ w_out


def apply_rope(x, cos, sin):
    half = x.shape[-1] // 2
    x1, x2 = x[..., :half], x[..., half:]
    return jnp.concatenate([x1 * cos - x2 * sin, x1 * sin + x2 * cos], axis=-1).astype(x.dtype)


def mla_mixer(h, positions, w_in, q_g, kv_g, w_uq, w_ukv, w_out):
    bsz, seq, _ = h.shape
    cq, ckv, kr = jnp.split(h @ w_in, [Q_LORA, Q_LORA + KV_LORA], axis=-1)
    q = (rms_norm(cq, q_g) @ w_uq).reshape(bsz, seq, C_HEADS, C_NOPE + C_ROPE)
    kv = (rms_norm(ckv, kv_g) @ w_ukv).reshape(bsz, seq, C_HEADS, C_NOPE + C_V)

    half = C_ROPE // 2
    freq = ROPE_BASE ** (-jnp.arange(half, dtype=jnp.float32) / half)
    ang = positions.astype(jnp.float32)[..., None] * freq
    cos = jnp.cos(ang)[:, :, None, :]
    sin = jnp.sin(ang)[:, :, None, :]
    q_rope = apply_rope(q[..., C_NOPE:], cos, sin)
    k_rope = apply_rope(kr[:, :, None, :], cos, sin)
    qf = jnp.concatenate([q[..., :C_NOPE], q_rope], axis=-1)
    kf = jnp.concatenate([kv[..., :C_NOPE],
                          jnp.broadcast_to(k_rope, (bsz, seq, C_HEADS, C_ROPE))], axis=-1)
    v = kv[..., C_NOPE:]
    scale = (C_NOPE + C_ROPE) ** -0.5

    qb = jnp.moveaxis(qf.reshape(bsz, seq // Q_BLOCK, Q_BLOCK, C_HEADS, C_NOPE + C_ROPE), 1, 0)

    def attend(qblk):
        s = jnp.einsum("bqhd,bkhd->bhqk", qblk, kf, preferred_element_type=jnp.float32) * scale
        pr = jax.nn.softmax(s, axis=-1)
        return jnp.einsum("bhqk,bkhd->bqhd", pr.astype(v.dtype), v)

    o = lax.map(attend, qb)
    o = jnp.moveaxis(o, 0, 1).reshape(bsz, seq, C_HEADS * C_V)
    return o @ w_out


def setup_inputs(seed: int = 0) -> dict:
    key = jax.random.key(seed)
    ks = iter(jax.random.split(key, 40))
    f32 = jnp.float32

    def nrm(shape, scale):
        return jax.random.normal(next(ks), shape, f32) * scale

    def gain(shape):
        return 1.0 + 0.02 * jax.random.normal(next(ks), shape, f32)

    x = jax.random.normal(next(ks), (BATCH, SEQ, D_MODEL), f32)
    offsets = jax.random.randint(next(ks), (BATCH, 1), 0, 1024, dtype=jnp.int32)
    positions = offsets + jnp.arange(SEQ, dtype=jnp.int32)[None, :]

    forget_b = jnp.linspace(3.0, 6.0, A_HEADS, dtype=f32)
    gate_b = jnp.concatenate([
        nrm((N_EVEN, A_HEADS), 0.1),
        forget_b + nrm((N_EVEN, A_HEADS), 0.1),
        nrm((N_EVEN, A_HEADS), 0.1),
        forget_b + nrm((N_EVEN, A_HEADS), 0.1)], axis=-1)

    return {
        "x": x,
        "positions": positions,
        "ab_norm": gain((N_EVEN, D_MODEL)),
        "ab_w_in": nrm((N_EVEN, D_MODEL, AB_IN), D_MODEL ** -0.5),
        "ab_conv": nrm((N_EVEN, CONV_WIDTH, 2 * A_WIDTH), CONV_WIDTH ** -0.5),
        "ab_gate_b": gate_b,
        "ab_head_g": gain((N_EVEN, A_WIDTH)),
        "ab_v_g": gain((N_EVEN, B_WIDTH)),
        "ab_ws": nrm((N_EVEN, B_GROUPS, SGU_CHUNK, SGU_CHUNK), SGU_CHUNK ** -0.5),
        "ab_bs": gain((N_EVEN, B_GROUPS, SGU_CHUNK)),
        "ab_w_out": nrm((N_EVEN, A_WIDTH + B_WIDTH, D_MODEL), (A_WIDTH + B_WIDTH) ** -0.5),
        "c_norm": gain((N_ODD, D_MODEL)),
        "c_w_in": nrm((N_ODD, D_MODEL, C_IN), D_MODEL ** -0.5),
        "c_q_g": gain((N_ODD, Q_LORA)),
        "c_kv_g": gain((N_ODD, KV_LORA)),
        "c_w_uq": nrm((N_ODD, Q_LORA, C_HEADS * (C_NOPE + C_ROPE)), Q_LORA ** -0.5),
        "c_w_ukv": nrm((N_ODD, KV_LORA, C_HEADS * (C_NOPE + C_V)), KV_LORA ** -0.5),
        "c_w_out": nrm((N_ODD, C_HEADS * C_V, D_MODEL), (C_HEADS * C_V) ** -0.5),
        "ffn_norm": gain((DEPTH, D_MODEL)),
        "ffn_w1": nrm((DEPTH, D_MODEL, D_FF), D_MODEL ** -0.5),
        "ffn_w2": nrm((DEPTH, D_FF, D_MODEL), D_FF ** -0.5),
        "final_norm": gain((D_MODEL,)),
    }


def reference(x, positions, ab_norm, ab_w_in, ab_conv, ab_gate_b, ab_head_g, ab_v_g,
              ab_ws, ab_bs, ab_w_out, c_norm, c_w_in, c_q_g, c_kv_g, c_w_uq, c_w_ukv,
              c_w_out, ffn_norm, ffn_w1, ffn_w2, final_norm):
    for layer in range(DEPTH):
        j = layer // 2
        if layer % 2 == 0:
            x = x + ab_mixer(rms_norm(x, ab_norm[j]), ab_w_in[j], ab_conv[j], ab_gate_b[j],
                             ab_head_g[j], ab_v_g[j], ab_ws[j], ab_bs[j], ab_w_out[j])
        else:
            x = x + mla_mixer(rms_norm(x, c_norm[j]), positions, c_w_in[j], c_q_g[j],
                              c_kv_g[j], c_w_uq[j], c_w_ukv[j], c_w_out[j])
        hf = rms_norm(x, ffn_norm[layer])
        x = x + jnp.square(jax.nn.relu(hf @ ffn_w1[layer])) @ ffn_w2[layer]
    return rms_norm(x, final_norm)
```

```python
import numpy as np
import concourse.bass as bass
import concourse.mybir as mybir

F32 = mybir.dt.float32
BF16 = mybir.dt.bfloat16
I32 = mybir.dt.int32
AF = mybir.ActivationFunctionType
ALU = mybir.AluOpType
AX = mybir.AxisListType


class Op:
    __slots__ = ("stream", "comp", "pos", "fn", "waits", "signal", "vc", "isdma", "sigval")


class Sch:
    STREAMS = ("pe", "act", "dve", "pool", "sp")

    def __init__(self, ndma=16):
        self.ops = {s: [] for s in self.STREAMS}
        self.comp_ops = {}
        self.last_w = {}
        self.readers = {}
        self.clock = {s: {} for s in self.STREAMS}
        self.ndma = {"sp": ndma, "pool": 8, "act": 4}
        self.dma_rr = {"sp": 0, "pool": 0, "act": 0}
        self.nops = 0

    def add(self, stream, fn, r=(), w=(), dma=False):
        op = Op()
        op.stream = stream
        op.fn = fn
        op.isdma = dma
        op.signal = dma
        op.waits = []
        deps = []
        if any(isinstance(t, str) and t.startswith("ps") and t[2:].isdigit() for t in r):
            w = list(w) + [t for t in r if isinstance(t, str) and t.startswith("ps") and t[2:].isdigit()]
        for t in r:
            x = self.last_w.get(t)
            if x is not None:
                deps.append((x, 0))
        for t in w:
            x = self.last_w.get(t)
            if x is not None:
                deps.append((x, 1))
            for x in self.readers.get(t, ()):
                deps.append((x, 2))
        if dma:
            k = self.dma_rr[stream]
            self.dma_rr[stream] = (k + 1) % self.ndma[stream]
            op.comp = "dma_%s%d" % (stream, k)
            lst = self.comp_ops.setdefault(op.comp, [])
            if lst:
                deps.append((lst[-1], 3))
        else:
            op.comp = stream
            lst = self.comp_ops.setdefault(op.comp, [])
        op.pos = len(lst)
        lst.append(op)
        clk = self.clock[stream]
        for x, kind in sorted(deps, key=lambda d: -d[0].pos):
            if x is op:
                continue
            if x.comp == stream and not x.isdma:
                if kind != 0 or stream == "pe":
                    continue
            if clk.get(x.comp, -1) >= x.pos:
                continue
            op.waits.append(x)
            x.signal = True
            for e, p in x.vc.items():
                if clk.get(e, -1) < p:
                    clk[e] = p
        op.vc = dict(clk)
        op.vc[op.comp] = op.pos
        for t in r:
            self.readers.setdefault(t, []).append(op)
        for t in w:
            self.last_w[t] = op
            self.readers[t] = []
        self.ops[stream].append(op)
        self.nops += 1
        return op

    def barrier(self):
        lasts = [lst[-1] for lst in self.comp_ops.values() if lst]
        for s in self.STREAMS:
            op = Op()
            op.stream = s
            op.fn = None
            op.isdma = False
            op.signal = False
            op.waits = []
            op.comp = s
            lst = self.comp_ops.setdefault(s, [])
            op.pos = len(lst)
            clk = self.clock[s]
            for x in sorted(lasts, key=lambda d: -d.pos):
                if clk.get(x.comp, -1) >= x.pos:
                    continue
                op.waits.append(x)
                x.signal = True
                for e, p in x.vc.items():
                    if clk.get(e, -1) < p:
                        clk[e] = p
            op.pos = len(lst) - 1
            op.vc = dict(clk)
            self.ops[s].append(op)
        self.last_w = {}
        self.readers = {}

    def emit(self, nc):
        import contextlib
        for comp, lst in self.comp_ops.items():
            n = 0
            for op in lst:
                if op.signal:
                    n += 16 if op.isdma else 1
                op.sigval = n
        comps = list(self.comp_ops.keys())
        with contextlib.ExitStack() as es:
            sems = {c: es.enter_context(nc.semaphore("s_" + c)) for c in comps}
            block = es.enter_context(nc.Block())

            def run(eng, ops):
                for op in ops:
                    for x in op.waits:
                        eng.wait_ge(sems[x.comp], x.sigval)
                    if op.fn is None:
                        continue
                    ins = op.fn(eng)
                    if op.signal:
                        ins.then_inc(sems[op.comp], 16 if op.isdma else 1)

            @block.tensor
            def _(e):
                run(e, self.ops["pe"])

            @block.scalar
            def _(e):
                run(e, self.ops["act"])

            @block.vector
            def _(e):
                run(e, self.ops["dve"])

            @block.gpsimd
            def _(e):
                run(e, self.ops["pool"])

            @block.sync
            def _(e):
                run(e, self.ops["sp"])

    def mm(self, out, lhsT, rhs, start=True, stop=True, r=(), w=(), **kw):
        return self.add("pe", lambda e: e.matmul(out, lhsT, rhs, start=start, stop=stop, **kw), r, w)

    def tr(self, out, in_, ident, r=(), w=()):
        return self.add("pe", lambda e: e.transpose(out, in_, ident), r, w)

    def act(self, out, in_, func, bias=None, scale=None, accum_out=None, r=(), w=(), eng="act"):
        kw = {}
        if bias is not None:
            kw["bias"] = bias
        if scale is not None:
            kw["scale"] = scale
        if accum_out is not None:
            kw["accum_out"] = accum_out
        return self.add(eng, lambda e: e.activation(out, in_, func, **kw), r, w)

    def tt(self, eng, out, in0, in1, op, r=(), w=()):
        return self.add(eng, lambda e: e.tensor_tensor(out, in0, in1, op), r, w)

    def ts(self, eng, out, in0, s1, s2, op0, op1=None, r=(), w=(), accum_out=None):
        if op1 is None:
            return self.add(eng, lambda e: e.tensor_scalar(out, in0, s1, None, op0), r, w)
        if accum_out is not None:
            return self.add(eng, lambda e: e.tensor_scalar(out, in0, s1, s2, op0, op1, accum_out=accum_out), r, w)
        return self.add(eng, lambda e: e.tensor_scalar(out, in0, s1, s2, op0, op1), r, w)

    def stt(self, eng, out, in0, scalar, in1, op0, op1, r=(), w=()):
        return self.add(eng, lambda e: e.scalar_tensor_tensor(out, in0, scalar, in1, op0, op1), r, w)

    def cp(self, eng, out, in_, r=(), w=()):
        if eng == "act":
            return self.add(eng, lambda e: e.copy(out, in_), r, w)
        return self.add(eng, lambda e: e.tensor_copy(out, in_), r, w)

    def memset(self, eng, ap, val, w=()):
        return self.add(eng, lambda e: e.memset(ap, val), (), w)

    def dma(self, out, in_, r=(), w=(), q="sp", **kw):
        return self.add(q, lambda e: e.dma_start(out=out, in_=in_, **kw), r, w, dma=True)

import contextlib
from concourse.bass_utils import run_bass_kernel_spmd

S = 4096
D = 1024
TT = 512
NT = 8
EPS = 1e-6
ATT_SCALE = 96 ** -0.5
QSCALE = 128 ** -0.5
TWO_PI = 6.283185307179586


def _layout(items):
    off = {}
    o = 0
    for name, n in items:
        off[name] = (o, n)
        o += n
    return off, o


PRM_ITEMS = []
for _j in range(2):
    PRM_ITEMS += [("abn%d" % _j, 8), ("cn%d" % _j, 8), ("conv%d" % _j, 40), ("gb%d" % _j, 64),
                  ("hgc%d" % _j, 4), ("vg%d" % _j, 4), ("wsT%d" % _j, 512), ("bs%d" % _j, 512),
                  ("qg%d" % _j, 3), ("kvg%d" % _j, 2)]
for _l in range(4):
    PRM_ITEMS += [("fn%d" % _l, 8)]
PRM_ITEMS += [("finn", 8)]
PRM_OFF, NPRM = _layout(PRM_ITEMS)
CST_ITEMS = [("ident", 128), ("maskF", 128), ("maskB", 128), ("ones", 128), ("sel", 128),
             ("fq", 1), ("ph", 1), ("fk", 1), ("pk", 1)]
CST_OFF, NCST = _layout(CST_ITEMS)


def host_pack(inp):
    prm = np.zeros((128, NPRM), np.float32)

    def put(name, arr):
        o, n = PRM_OFF[name]
        prm[:, o:o + n] = np.asarray(arr, np.float32).reshape(128, n)

    def colmaj(v, nch):
        return np.asarray(v).reshape(nch, 128).T

    for j in range(2):
        put("abn%d" % j, colmaj(inp["ab_norm"][j], 8))
        put("cn%d" % j, colmaj(inp["c_norm"][j], 8))
        cv = np.asarray(inp["ab_conv"][j])
        put("conv%d" % j, cv.T.reshape(8, 128, 5).transpose(1, 0, 2).reshape(128, 40))
        put("gb%d" % j, np.tile(np.asarray(inp["ab_gate_b"][j])[None, :], (128, 4)))
        put("hgc%d" % j, colmaj(inp["ab_head_g"][j], 4))
        put("vg%d" % j, colmaj(inp["ab_v_g"][j], 4))
        ws = np.asarray(inp["ab_ws"][j])
        put("wsT%d" % j, ws.transpose(2, 0, 1).reshape(128, 512))
        bs = np.asarray(inp["ab_bs"][j])
        put("bs%d" % j, np.tile(bs.reshape(1, 512), (128, 1)))
        put("qg%d" % j, colmaj(inp["c_q_g"][j], 3))
        put("kvg%d" % j, colmaj(inp["c_kv_g"][j], 2))
    for l in range(4):
        put("fn%d" % l, colmaj(inp["ffn_norm"][l], 8))
    put("finn", colmaj(inp["final_norm"], 8))

    cst = np.zeros((128, NCST), np.float32)

    def putc(name, arr):
        o, n = CST_OFF[name]
        cst[:, o:o + n] = np.asarray(arr, np.float32).reshape(128, n)

    putc("ident", np.eye(128))
    ii = np.arange(128)
    putc("maskF", (ii[:, None] <= ii[None, :]).astype(np.float32))
    putc("maskB", (ii[:, None] >= ii[None, :]).astype(np.float32))
    putc("ones", np.ones((128, 128)))
    sel = np.zeros((128, 128), np.float32)
    for jj in range(32):
        sel[jj, 64 + jj] = 1; sel[jj, 96 + jj] = 1
        sel[32 + jj, 64 + jj] = 1; sel[32 + jj, 96 + jj] = 1
    putc("sel", sel)
    half = 16
    freq = (10000.0 ** (-np.arange(half, dtype=np.float32) / half)).astype(np.float32)
    f2 = (freq.astype(np.float64) / TWO_PI).astype(np.float32)
    fq = np.zeros(128, np.float32); ph = np.zeros(128, np.float32)
    ph[0:64] = 0.25
    for p in range(64, 96):
        fq[p] = f2[(p - 64) % 16]; ph[p] = 0.25
    for p in range(96, 128):
        fq[p] = f2[(p - 96) % 16]; ph[p] = 0.0
    fk = np.zeros(128, np.float32); pk = np.zeros(128, np.float32)
    for p in range(0, 32):
        fk[p] = f2[p % 16]; pk[p] = 0.25
    for p in range(32, 64):
        fk[p] = f2[p % 16]; pk[p] = 0.0
    putc("fq", fq); putc("ph", ph); putc("fk", fk); putc("pk", pk)
    return prm, cst


class Arena:
    def __init__(self, ap):
        self.ap = ap
        self.off = 0
        self.cap = ap.shape[1]
        self.peak = 0

    def _take(self, w):
        w = (w + 7) // 8 * 8
        o = self.off
        self.off += w
        self.peak = max(self.peak, self.off)
        assert self.off <= self.cap, ("arena overflow", self.off, self.cap)
        return o, w

    def f32(self, n):
        o, w = self._take(n)
        return self.ap[:, o:o + n]

    def bf(self, n):
        o, w = self._take((n + 1) // 2)
        return self.ap[:, o:o + w].bitcast(BF16)[:, 0:n]

    def i32(self, n):
        o, w = self._take(n)
        return self.ap[:, o:o + n].bitcast(I32)

    def mark(self):
        return self.off

    def release(self, m):
        self.off = m


class Rot:
    def __init__(self, items):
        self.items = items
        self.i = 0

    def next(self):
        it = self.items[self.i % len(self.items)]
        self.i += 1
        return it


def needed_weights(blocks):
    need = set()
    for b in blocks:
        if b.startswith("ffn"):
            need |= {"ffn_w1", "ffn_w2"}
        elif b.startswith("ab"):
            need |= {"ab_w_in", "ab_w_out"}
        elif b.startswith("mla"):
            need |= {"c_w_in", "c_w_uq", "c_w_ukv", "c_w_out"}
    return need


DEBUG = {}


ALL_BLOCKS = ["load", "ab0", "ffn0", "mla0", "ffn1", "ab1", "ffn2", "mla1", "ffn3", "final"]


def build(blocks=None, debug_out=None):
    blocks = list(ALL_BLOCKS if blocks is None else blocks)
    nc = bass.Bass("TRN2", target_bir_lowering=False)
    s = Sch()

    def din(name, shape, dt=F32):
        return nc.dram_tensor(name, list(shape), dt, kind="ExternalInput").ap()

    x_in = din("xT", [8, 128, S])
    posb = din("posb", [128, S], I32)
    prm_d = din("prm", [128, NPRM])
    cst_d = din("cst", [128, NCST])
    need = needed_weights(blocks)
    dw = lambda name, shape: din(name, shape) if name in need else None
    ab_w_in = dw("ab_w_in", [2, 1024, 3088])
    ab_w_out = dw("ab_w_out", [2, 1024, 1024])
    c_w_in = dw("c_w_in", [2, 1024, 672])
    c_w_uq = dw("c_w_uq", [2, 384, 1536])
    c_w_ukv = dw("c_w_ukv", [2, 256, 2048])
    c_w_out = dw("c_w_out", [2, 1024, 1024])
    ffn_w1 = dw("ffn_w1", [4, 1024, 4096])
    ffn_w2 = dw("ffn_w2", [4, 4096, 1024])
    out_d = nc.dram_tensor("outT", [8, 128, S], F32, kind="ExternalOutput").ap()

    def scratch(name, shape, dt=BF16):
        return nc.dram_tensor(name, list(shape), dt).ap()

    XS = [scratch("xs0", [8, 128, S], F32), scratch("xs1", [8, 128, S], F32)]
    W1s = [scratch("w1s%d" % l, [8, 128, 8, 512]) for l in range(4)]
    W2s = [scratch("w2s%d" % l, [4, 128, 32, 256]) for l in range(4)]
    ABqkv = [scratch("abqkv%d" % j, [4, 128, 8, 3, 128]) for j in range(2)]
    ABrest = [scratch("abrest%d" % j, [128, 8, 1552]) for j in range(2)]
    ABout = [scratch("about%d" % j, [128, 8, 1024]) for j in range(2)]
    Ccin = [scratch("ccin%d" % j, [128, 8, 672]) for j in range(2)]
    Cuq = [scratch("cuq%d" % j, [128, 3, 1536]) for j in range(2)]
    Cukv = [scratch("cukv%d" % j, [128, 2, 2048]) for j in range(2)]
    Cout = [scratch("cout%d" % j, [128, 8, 1024]) for j in range(2)]

    es = contextlib.ExitStack()
    with es:
        arena_t = es.enter_context(nc.sbuf_tensor("arena", [128, 49152], F32))
        prm = es.enter_context(nc.sbuf_tensor("prm_sb", [128, NPRM], F32))
        cst = es.enter_context(nc.sbuf_tensor("cst_sb", [128, NCST], F32))
        cstb = es.enter_context(nc.sbuf_tensor("cstb_sb", [128, 256], BF16))
        PSt = [es.enter_context(nc.psum_tensor("ps%d" % k, [128, 512], F32)) for k in range(8)]
        PS = [(PSt[k][:], "ps%d" % k) for k in range(8)]
        A = Arena(arena_t[:])

        def P(name, a=None, b=None):
            o, n = PRM_OFF[name]
            a = 0 if a is None else a
            b = n if b is None else b
            return prm[:, o + a:o + b]

        def C(name, a=None, b=None):
            o, n = CST_OFF[name]
            a = 0 if a is None else a
            b = n if b is None else b
            return cst[:, o + a:o + b]

        ident = C("ident")
        ident_bf = cstb[:, 0:128]
        ones_bf = cstb[:, 128:256]

        s.dma(prm[:], prm_d, w=["prm"])
        s.dma(cst[:], cst_d, w=["cst"])
        s.cp("dve", ident_bf, C("ident"), r=["cst"], w=["cstb"])
        s.cp("dve", ones_bf, C("ones"), r=["cst"], w=["cstb"])

        def conv_ffn(l):
            for pc in range(8):
                s.dma(W1s[l][pc], ffn_w1[l][:, pc * 512:(pc + 1) * 512].rearrange("(kc p) n -> p kc n", p=128),
                      w=["W1s%d_%d" % (l, pc)], q="pool")
            for pc in range(4):
                s.dma(W2s[l][pc], ffn_w2[l][:, pc * 256:(pc + 1) * 256].rearrange("(kc p) n -> p kc n", p=128),
                      w=["W2s%d_%d" % (l, pc)], q="pool")

        def conv_ab(j):
            for h in range(4):
                for t in range(3):
                    s.dma(ABqkv[j][h][:, :, t, :],
                          ab_w_in[j][:, t * 512 + h * 128: t * 512 + (h + 1) * 128].rearrange("(kc p) n -> p kc n", p=128),
                          w=["ABqkv%d_%d_%d" % (j, h, t)], q="pool")
            s.dma(ABrest[j], ab_w_in[j][:, 1536:3088].rearrange("(kc p) n -> p kc n", p=128), w=["ABrest%d" % j], q="pool")
            s.dma(ABout[j], ab_w_out[j].rearrange("(kc p) n -> p kc n", p=128), w=["ABout%d" % j], q="pool")

        def conv_mla(j):
            s.dma(Ccin[j], c_w_in[j].rearrange("(kc p) n -> p kc n", p=128), w=["Ccin%d" % j], q="pool")
            s.dma(Cuq[j], c_w_uq[j].rearrange("(kc p) n -> p kc n", p=128), w=["Cuq%d" % j], q="pool")
            s.dma(Cukv[j], c_w_ukv[j].rearrange("(kc p) n -> p kc n", p=128), w=["Cukv%d" % j], q="pool")
            s.dma(Cout[j], c_w_out[j].rearrange("(kc p) n -> p kc n", p=128), w=["Cout%d" % j], q="pool")

        def conv_for(b):
            if b.startswith("ffn"):
                conv_ffn(int(b[3:]))
            elif b.startswith("ab"):
                conv_ab(int(b[2:]))
            elif b.startswith("mla"):
                conv_mla(int(b[3:]))

        def Xtile(X, i):
            return X[:, :, i * TT:(i + 1) * TT].rearrange("c p t -> p c t")

        def blk_load(Y):
            m = A.mark()
            xin = [A.f32(4096) for _ in range(2)]
            st = [A.f32(4096) for _ in range(2)]
            psr = Rot(PS)
            for i in range(NT):
                b = i % 2
                xi = xin[b].rearrange("p (a n) -> p a n", a=4)
                s.dma(xi, x_in[i * 512:(i + 1) * 512, :].rearrange("(a p) n -> p a n", p=128), w=["xin%d" % b])
                sv = st[b].rearrange("p (c t) -> p c t", c=8)
                for c in range(8):
                    ps, pt = psr.next()
                    for a in range(4):
                        s.tr(ps[:, a * 128:(a + 1) * 128], xi[:, a, c * 128:(c + 1) * 128], ident,
                             r=["xin%d" % b, "cst"], w=[pt])
                    s.cp("dve" if c % 2 == 0 else "act", sv[:, c, :], ps, r=[pt], w=["st%d_%d" % (b, c)])
                s.dma(Xtile(Y, i), sv, r=["st%d_%d" % (b, c) for c in range(8)], w=["X%d" % i])
            A.release(m)
            s.barrier()

        def load_x(X, i, xt, xt_tok):
            s.dma(xt, Xtile(X, i), r=["X%d" % i], w=[xt_tok])

        def norm_tile(X, i, g, xt, xt_tok, hn, hn_tok, sq, rs, preloaded=False):
            if not preloaded:
                load_x(X, i, xt, xt_tok)
            for c in range(8):
                s.act(sq[:, c, :], xt[:, c, :], AF.Square, r=[xt_tok], w=["sq%d" % c])
            ps, pt = PS[0]
            for c in range(8):
                s.mm(ps, ones_bf, sq[:, c, :], start=(c == 0), stop=(c == 7), r=["sq%d" % c, "cstb"], w=[pt])
            s.ts("dve", rs, ps, 1.0 / D, EPS, ALU.mult, ALU.add, r=[pt], w=["rs"])
            s.act(rs, rs, AF.Ln, r=["rs"], w=["rs"])
            s.act(rs, rs, AF.Exp, scale=-0.5, r=["rs"], w=["rs"])
            for c in range(8):
                s.stt("dve", hn[:, c, :], xt[:, c, :], g[:, c:c + 1], rs, ALU.mult, ALU.mult,
                      r=[xt_tok, "rs", "prm"], w=[hn_tok])

        def blk_ffn(l, X, Y):
            m = A.mark()
            xts = [A.f32(4096).rearrange("p (c t) -> p c t", c=8) for _ in range(2)]
            hns = [A.bf(4096).rearrange("p (c t) -> p c t", c=8) for _ in range(2)]
            sq = A.bf(4096).rearrange("p (c t) -> p c t", c=8)
            rs = A.f32(512)
            hb = A.bf(32 * 512).rearrange("p (c t) -> p c t", c=32)
            NW1, NW2 = 4, 3
            w1b = [A.bf(8 * 512).rearrange("p (k n) -> p k n", k=8) for _ in range(NW1)]
            w2b = [A.bf(32 * 256).rearrange("p (k n) -> p k n", k=32) for _ in range(NW2)]
            r32 = [A.f32(512) for _ in range(2)]
            g = P("fn%d" % l)
            ps1 = Rot(PS[1:4])
            ps2 = Rot(PS[4:8])
            n1 = 0
            n2 = 0
            nr = 0
            norm_tile(X, 0, g, xts[0], "xt0", hns[0], "hn0", sq, rs)
            for i in range(NT):
                b = i % 2
                xt, hn = xts[b], hns[b]
                for pc in range(8):
                    wb = n1 % NW1
                    n1 += 1
                    s.dma(w1b[wb], W1s[l][pc], r=["W1s%d_%d" % (l, pc)], w=["w1b%d" % wb])
                    for mm_ in range(4):
                        ps, pt = ps1.next()
                        for kc in range(8):
                            s.mm(ps, w1b[wb][:, kc, mm_ * 128:(mm_ + 1) * 128], hn[:, kc, :],
                                 start=(kc == 0), stop=(kc == 7), r=["w1b%d" % wb, "hn%d" % b], w=[pt])
                        rb = nr % 2
                        nr += 1
                        s.act(r32[rb], ps, AF.Relu, r=[pt], w=["r32%d" % rb])
                        s.tt("dve", hb[:, pc * 4 + mm_, :], r32[rb], r32[rb], ALU.mult,
                             r=["r32%d" % rb], w=["hb%d" % (pc * 4 + mm_)])
                    if pc == 3 and i + 1 < NT:
                        nb_ = (i + 1) % 2
                        norm_tile(X, i + 1, g, xts[nb_], "xt%d" % nb_, hns[nb_], "hn%d" % nb_, sq, rs)
                for pc in range(4):
                    wb = n2 % NW2
                    n2 += 1
                    s.dma(w2b[wb], W2s[l][pc], r=["W2s%d_%d" % (l, pc)], w=["w2b%d" % wb])
                    for jj in range(2):
                        j = pc * 2 + jj
                        ps, pt = ps2.next()
                        for kc in range(32):
                            s.mm(ps, w2b[wb][:, kc, jj * 128:(jj + 1) * 128], hb[:, kc, :],
                                 start=(kc == 0), stop=(kc == 31), r=["w2b%d" % wb, "hb%d" % kc], w=[pt])
                        s.tt("dve", xt[:, j, :], ps, xt[:, j, :], ALU.add, r=[pt, "xt%d" % b], w=["xo%d_%d" % (b, j)])
                s.dma(Xtile(Y, i), xt, r=["xo%d_%d" % (b, j) for j in range(8)] + ["xt%d" % b],
                      w=["X%d" % i, "xt%d" % b])
            A.release(m)
            s.barrier()

        def blk_final(X):
            m = A.mark()
            xts = [A.f32(4096).rearrange("p (c t) -> p c t", c=8) for _ in range(2)]
            hns = [A.f32(4096).rearrange("p (c t) -> p c t", c=8) for _ in range(2)]
            sq = A.bf(4096).rearrange("p (c t) -> p c t", c=8)
            rs = A.f32(512)
            g = P("finn")
            load_x(X, 0, xts[0], "xt0")
            for i in range(NT):
                b = i % 2
                xt, hn = xts[b], hns[b]
                if i + 1 < NT:
                    load_x(X, i + 1, xts[1 - b], "xt%d" % (1 - b))
                norm_tile(X, i, g, xt, "xt%d" % b, hn, "hn%d" % b, sq, rs, preloaded=True)
                s.dma(Xtile(out_d, i), hn, r=["hn%d" % b], w=["out%d" % i, "hn%d" % b])
            s.add("sp", None, r=["out%d" % i for i in range(NT)])
            A.release(m)

        def gelu(out, ps, pt, gt, otok):
            s.act(out, ps, AF.Gelu_apprx_tanh, r=[pt], w=[otok])

        def blk_ab(j, X, Y):
            m0 = A.mark()
            if DEBUG.get("ab_stop") == 0:
                s.barrier()
                return False
            HN = A.bf(8 * S).rearrange("p (c t) -> p c t", c=8)
            HAT = A.bf(4 * S).rearrange("p (c t) -> p c t", c=4)
            mH = A.mark()
            G = A.f32(32 * 16)
            NLF = A.f32(256)
            EB = A.f32(256); CF = A.f32(256); WW = A.f32(256); DEC = A.f32(256); EB2 = A.f32(256)
            v4 = lambda ap: ap.rearrange("p (d c h) -> p d c h", d=2, c=32)
            m1 = A.mark()
            xts = [A.f32(4096).rearrange("p (c t) -> p c t", c=8) for _ in range(2)]
            sq = A.bf(4096).rearrange("p (c t) -> p c t", c=8)
            rs = A.f32(512)
            wg = A.bf(8 * 16).rearrange("p (k n) -> p k n", k=8)
            s.dma(wg, ABrest[j][:, :, 512:528], r=["ABrest%d" % j], w=["wg"])
            g = P("abn%d" % j)
            psg = Rot(PS[1:3])
            Gv = G.rearrange("p (c n) -> p c n", n=16)
            for i in range(NT):
                b = i % 2
                norm_tile(X, i, g, xts[b], "xt%d" % b, HN[:, :, i * TT:(i + 1) * TT], "HN%d" % i, sq, rs)
                ps, pt = psg.next()
                for tc in range(4):
                    ch = i * 4 + tc
                    for kc in range(8):
                        s.mm(ps[:, tc * 16:(tc + 1) * 16], HN[:, kc, ch * 128:(ch + 1) * 128], wg[:, kc, :],
                             start=(kc == 0), stop=(kc == 7), r=["HN%d" % i, "wg"], w=[pt])
                s.tt("dve", G[:, i * 64:(i + 1) * 64], ps[:, 0:64], P("gb%d" % j), ALU.add, r=[pt, "prm"], w=["G"])
            if DEBUG.get("ab_stop") == 0.5:
                A.release(m0)
                s.barrier()
                return False
            G4 = G.rearrange("p (c g h) -> p c g h", g=4, h=4)
            NLF4 = v4(NLF)
            tmpg = A.f32(256)
            tmp4 = v4(tmpg)
            for d in range(2):
                s.act(tmp4[:, d], G4[:, :, 2 * d + 1, :], AF.Exp, scale=-1.0, r=["G"], w=["tmpg"])
            s.act(NLF, tmpg, AF.Ln, bias=1.0, r=["tmpg"], w=["NLF"])
            if DEBUG.get("ab_stop") == 0.6:
                A.release(m0)
                s.barrier()
                return False
            pc_, pct = PS[1]
            s.mm(pc_[:, 0:128], C("maskF"), NLF[:, 0:128], r=["cst", "NLF"], w=[pct])
            s.mm(pc_[:, 128:256], C("maskB"), NLF[:, 128:256], r=["cst", "NLF"], w=[pct])
            pg_, pgt = PS[2]
            s.mm(pg_[:, 0:256], C("ones"), NLF, r=["cst", "NLF"], w=[pgt])
            if DEBUG.get("ab_stop") == 0.7:
                s.cp("dve", EB, pc_[:, 0:256], r=[pct], w=["EB"])
                s.cp("dve", DEC, pg_[:, 0:256], r=[pgt], w=["DEC"])
                A.release(m0)
                s.barrier()
                return False
            s.act(EB, pc_[:, 0:256], AF.Exp, scale=-1.0, r=[pct], w=["EB"])
            s.act(DEC, pg_[:, 0:256], AF.Exp, scale=-1.0, r=[pgt], w=["DEC"])
            tl = A.f32(256)
            tl4 = v4(tl)
            pc4 = v4(pc_[:, 0:256])
            LI = A.f32(256)
            LI4 = v4(LI)
            for d in range(2):
                s.cp("act", LI4[:, d], G4[:, :, 2 * d, :], r=["G"], w=["LI"])
            s.tt("dve", tl, pc_[:, 0:256], LI, ALU.add, r=[pct, "LI"], w=["tl"])
            s.act(CF, tl, AF.Exp, r=["tl"], w=["CF"])
            tl2 = A.f32(256)
            s.tt("dve", tl2, tl, pg_[:, 0:256], ALU.subtract, r=["tl", pgt], w=["tl2"])
            s.act(WW, tl2, AF.Exp, r=["tl2"], w=["WW"])
            EB4, CF4, WW4, DEC4 = v4(EB), v4(CF), v4(WW), v4(DEC)
            EB2v = EB2.rearrange("p (h c d) -> p c h d", c=32, h=4)
            for d in range(2):
                s.cp("act", EB2v[:, :, :, d], EB4[:, d], r=["EB"], w=["EB2"])
            A.release(m1)
            s.barrier()
            if DEBUG.get("ab_stop") == 1:
                A.release(m0)
                return False
            m2 = A.mark()
            raw = A.f32(S + 8)
            acc = A.f32(S)
            QT = A.bf(S); KT = A.bf(S)
            VE = A.bf(32 * 130).rearrange("p (c n) -> p c n", n=130)
            KK = A.bf(32 * 128).rearrange("p (c n) -> p c n", n=128)
            CST = A.bf(33 * 130).rearrange("p (c n) -> p c n", n=130)
            wq = A.bf(8 * 3 * 128).rearrange("p (k t n) -> p k t n", k=8, t=3)
            C32 = [A.f32(130) for _ in range(2)]
            C32b = [A.f32(130) for _ in range(2)]
            vws = [A.bf(130) for _ in range(4)]
            Sf = [A.bf(128) for _ in range(4)]
            Sb = [A.bf(128) for _ in range(4)]
            sm = [A.f32(32) for _ in range(5)]
            tmpo = [A.f32(128) for _ in range(2)]
            hs = [acc[:, 2080 + i_ * 128:2080 + (i_ + 1) * 128] for i_ in range(6)]
            sqj = [acc[:, 2080 + 768 + i_ * 128:2080 + 768 + (i_ + 1) * 128] for i_ in range(2)]
            VF = [acc[:, 3104 + i_ * 72:3104 + i_ * 72 + 65].bitcast(BF16) for i_ in range(4)]
            VB = [acc[:, 3104 + 288 + i_ * 72:3104 + 288 + i_ * 72 + 65].bitcast(BF16) for i_ in range(4)]
            S32 = [A.f32(256) for _ in range(2)]
            hnm = [A.bf(128) for _ in range(2)]
            s.memset("pool", raw, 0.0, w=["raw"])
            s.memset("pool", VE[:, :, 128:130], 1.0, w=["VE"])
            s.memset("pool", CST[:, 0, :], 0.0, w=["CST0"])
            cw = P("conv%d" % j).rearrange("p (c t) -> p c t", t=5)
            psP = Rot(PS[0:4])
            for h in range(DEBUG.get("ab_heads", 4)):
                s.dma(wq, ABqkv[j][h], r=["ABqkv%d_%d_%d" % (j, h, t) for t in range(3)], w=["wq"])
                for t in range(2):
                    for i in range(NT):
                        ps, pt = psP.next()
                        for kc in range(8):
                            s.mm(ps, wq[:, kc, t, :], HN[:, kc, i * TT:(i + 1) * TT], start=(kc == 0), stop=(kc == 7),
                                 r=["wq", "HN%d" % i], w=[pt])
                        s.cp("act" if i % 2 else "dve", raw[:, 2 + i * TT:2 + (i + 1) * TT], ps, r=[pt, "raw"], w=["raw%d" % i])
                    rawt = ["raw%d" % i for i in range(NT)] + ["raw"]
                    if t == 0:
                        VT = KK.rearrange("p c n -> p (c n)")
                        for i in range(NT):
                            ps, pt = psP.next()
                            for kc in range(8):
                                s.mm(ps, wq[:, kc, 2, :], HN[:, kc, i * TT:(i + 1) * TT], start=(kc == 0), stop=(kc == 7),
                                     r=["wq", "HN%d" % i], w=[pt])
                            s.cp("act", VT[:, i * TT:(i + 1) * TT], ps, r=[pt], w=["KK%d" % i])
                        for c4 in range(8):
                            ps, pt = psP.next()
                            psb = ps.bitcast(BF16)
                            for cc in range(4):
                                ch = c4 * 4 + cc
                                s.tr(psb[:, cc * 128:(cc + 1) * 128], VT[:, ch * 128:(ch + 1) * 128], ident_bf, r=["KK%d" % c4, "cstb"], w=[pt])
                            s.cp("dve", VE[:, c4 * 4:(c4 + 1) * 4, 0:128], psb[:, 0:512].rearrange("p (a n) -> p a n", a=4),
                                 r=[pt, "VE"], w=["VE%d" % c4])
                    cc = t * 4 + h
                    s.act(acc, raw[:, 0:S], AF.Copy, scale=cw[:, cc, 0:1], r=rawt + ["prm"],
                          w=["acc"] + ["hs%d" % i_ for i_ in range(6)] + ["sqj0", "sqj1"]
                          + ["VF%d" % i_ for i_ in range(4)] + ["VB%d" % i_ for i_ in range(4)])
                    for tap in range(1, 5):
                        s.stt("dve", acc, raw[:, tap:tap + S], cw[:, cc, tap:tap + 1], acc, ALU.mult, ALU.add,
                              r=rawt + ["prm", "acc"], w=["acc"])
                    if t == 0:
                        s.act(acc, acc, AF.Silu, r=["acc"], w=["acc"])
                        s.ts("dve", QT, acc, QSCALE, None, ALU.mult, r=["acc"], w=["QT"])
                    else:
                        s.act(KT, acc, AF.Silu, r=["acc"], w=["KT"])
                for c4 in range(8):
                    ps, pt = psP.next()
                    psb = ps.bitcast(BF16)
                    for cc in range(4):
                        ch = c4 * 4 + cc
                        s.tr(psb[:, cc * 128:(cc + 1) * 128], KT[:, ch * 128:(ch + 1) * 128], ident_bf, r=["KT", "cstb"], w=[pt])
                    s.cp("dve", KK[:, c4 * 4:(c4 + 1) * 4, :], psb[:, 0:512].rearrange("p (a n) -> p a n", a=4), r=[pt], w=["KK%d" % c4])
                CSTb = acc[:, 0:2080].bitcast(BF16).rearrange("p (c n) -> p c n", n=130)
                s.memset("pool", C32[1], 0.0, w=["C32f1"])
                s.memset("pool", C32b[1], 0.0, w=["C32b1"])
                s.memset("pool", CSTb[:, 31, :], 0.0, w=["acc"])
                psU = Rot(PS[4:8])
                LAS = 2
                ust = {}

                def st_front(k):
                    for d in range(2):
                        c = k if d == 0 else 31 - k
                        vi = (k * 2 + d) % 4
                        vw = vws[vi]; vt = "vw%d" % vi
                        s.act(vw, VE[:, c, :], AF.Copy, scale=WW4[:, d, c, h:h + 1], r=["VE%d" % (c // 4), "VE", "WW"], w=[vt])
                        ps, pt = psU.next()
                        s.mm(ps[:, 0:130], KK[:, c, :], vw, r=["KK%d" % (c // 4), vt], w=[pt])
                        ust[(k, d)] = (ps, pt)

                def st_back(k):
                    for d in range(2):
                        c = k if d == 0 else 31 - k
                        Cx = C32 if d == 0 else C32b
                        cn = "C32f" if d == 0 else "C32b"
                        ps, pt = ust.pop((k, d))
                        cur_, prv_ = k % 2, (k - 1) % 2
                        s.stt("dve", Cx[cur_], Cx[prv_], DEC4[:, d, c, h:h + 1], ps[:, 0:130], ALU.mult, ALU.add,
                              r=["%s%d" % (cn, prv_), "DEC", pt], w=["%s%d" % (cn, cur_)])
                        if d == 0:
                            s.cp("pool", CST[:, c + 1, :], Cx[cur_], r=["%s%d" % (cn, cur_)], w=["CST%d" % (c + 1)])
                        elif c >= 1:
                            s.cp("pool", CSTb[:, c - 1, :], Cx[cur_], r=["%s%d" % (cn, cur_)], w=["acc"])

                for k in range(32 + LAS - 1):
                    if k < 32:
                        st_front(k)
                    if k >= LAS - 1:
                        st_back(k - (LAS - 1))
                GP = 2
                NG = 32 // GP
                psS = Rot(PS[0:2]); psH = Rot(PS[2:6]); psT = Rot(PS[6:8])
                hgc = P("hgc%d" % j)
                sA = {}
                sBk = {}

                def stageA(g):
                    ps, pt = psS.next()
                    sA[g] = (ps, pt)
                    for q in range(GP):
                        c = g * GP + q
                        cs = slice(c * 128, (c + 1) * 128)
                        s.mm(ps[:, q * 128:(q + 1) * 128], KT[:, cs], QT[:, cs], r=["KT", "QT"], w=[pt])
                    s32 = S32[g % 2]; s32t = "S32_%d" % (g % 2)
                    s.cp("act", s32, ps[:, 0:GP * 128], r=[pt], w=[s32t])
                    for q in range(GP):
                        c = g * GP + q
                        bi = c % 4
                        s.tt("pool", Sf[bi], s32[:, q * 128:(q + 1) * 128], C("maskF"), ALU.mult, r=[s32t, "cst"], w=["Sf%d" % bi])
                        s.tt("dve", Sb[bi], s32[:, q * 128:(q + 1) * 128], C("maskB"), ALU.mult, r=[s32t, "cst"], w=["Sb%d" % bi])
                        vet = ["VE%d" % (c // 4), "VE", "CF"]
                        s.act(VF[bi], VE[:, c, :], AF.Copy, scale=CF4[:, 0, c, h:h + 1], r=vet, w=["VF%d" % bi])
                        s.act(VB[bi], VE[:, c, :], AF.Copy, scale=CF4[:, 1, c, h:h + 1], r=vet, w=["VB%d" % bi])

                def stageB(g):
                    c0 = g * GP
                    smg = sm[g % 5]
                    smt = "sm%d" % (g % 5)
                    banks = []
                    for q in range(GP):
                        c = c0 + q
                        cs = slice(c * 128, (c + 1) * 128)
                        bi = c % 4
                        ph_, pht = psH.next()
                        banks.append((ph_, pht))
                        vet = ["VE%d" % (c // 4), "VE"]
                        s.mm(ph_[:, 0:130], Sf[bi], VF[bi], start=True, stop=False, r=["Sf%d" % bi, "VF%d" % bi], w=[pht])
                        s.mm(ph_[:, 0:130], QT[:, cs], CST[:, c, :], start=False, stop=True, r=["QT", "CST%d" % c], w=[pht])
                        s.mm(ph_[:, 130:260], Sb[bi], VB[bi], start=True, stop=False, r=["Sb%d" % bi, "VB%d" % bi], w=[pht])
                        s.mm(ph_[:, 130:260], QT[:, cs], CSTb[:, c, :], start=False, stop=True, r=["QT", "acc"], w=[pht])
                    for q in range(GP):
                        ph_, pht = banks[q]
                        den2 = ph_[:, 0:260].rearrange("p (a n) -> p a n", n=130)[:, :, 128]
                        s.cp("dve", smg[:, 2 * q:2 * q + 2], den2, r=[pht], w=[smt])
                    ebg = EB2[:, h * 64 + c0 * 2:h * 64 + (c0 + GP) * 2]
                    n2 = 2 * GP
                    s.tt("dve", smg[:, 4:4 + n2], smg[:, 0:n2], ebg, ALU.mult, r=[smt, "EB2"], w=[smt])
                    s.add("dve", lambda e, o=smg[:, 8:8 + n2], i_=smg[:, 4:4 + n2]: e.reciprocal(o, i_), r=[smt], w=[smt])
                    s.stt("dve", smg[:, 12:12 + n2], smg[:, 8:8 + n2], -1.0, smg[:, 8:8 + n2], ALU.mult, ALU.max, r=[smt], w=[smt])
                    s.stt("dve", smg[:, 16:16 + n2], smg[:, 12:12 + n2], 1.0, ebg, ALU.min, ALU.mult, r=[smt, "EB2"], w=[smt])
                    sBk[g] = banks

                def stageB1b(g):
                    c0 = g * GP
                    smg = sm[g % 5]
                    smt = "sm%d" % (g % 5)
                    banks = sBk.pop(g)
                    for q in range(GP):
                        c = c0 + q
                        bi = c % 4
                        hi_ = c % 6
                        ph_, pht = banks[q]
                        s.act(tmpo[q], ph_[:, 0:128], AF.Copy, scale=smg[:, 16 + 2 * q:17 + 2 * q], r=[pht, smt], w=["tmpo%d" % q])
                        s.stt("dve", hs[hi_], ph_[:, 130:258], smg[:, 17 + 2 * q:18 + 2 * q], tmpo[q], ALU.mult, ALU.add,
                              r=[pht, smt, "tmpo%d" % q], w=["hs%d" % hi_])
                        s.act(sqj[q], hs[hi_], AF.Square, accum_out=smg[:, 20 + q:21 + q], r=["hs%d" % hi_], w=["sqj%d" % q, smt])

                def stageB2(g):
                    c0 = g * GP
                    smg = sm[g % 5]
                    smt = "sm%d" % (g % 5)
                    s.ts("dve", smg[:, 22:22 + GP], smg[:, 20:20 + GP], 1.0 / 128, EPS, ALU.mult, ALU.add, r=[smt], w=[smt])
                    s.act(smg[:, 24:24 + GP], smg[:, 22:22 + GP], AF.Sqrt, r=[smt], w=[smt])
                    s.add("dve", lambda e, o=smg[:, 26:26 + GP], i_=smg[:, 24:24 + GP]: e.reciprocal(o, i_), r=[smt], w=[smt])

                def stageB2b(g):
                    c0 = g * GP
                    smg = sm[g % 5]
                    smt = "sm%d" % (g % 5)
                    ptp, ptt = psT.next()
                    ptb = ptp.bitcast(BF16)
                    for q in range(GP):
                        c = c0 + q
                        bi = c % 4
                        hi_ = c % 6
                        s.act(hnm[q], hs[hi_], AF.Copy, scale=smg[:, 26 + q:27 + q], r=["hs%d" % hi_, smt], w=["hnm%d" % q])
                        s.tr(ptb[:, q * 128:(q + 1) * 128], hnm[q], ident_bf, r=["hnm%d" % q, "cstb"], w=[ptt])
                    s.act(HAT[:, h, c0 * 128:(c0 + GP) * 128], ptb[:, 0:GP * 128], AF.Copy, scale=hgc[:, h:h + 1],
                          r=[ptt, "prm"], w=["HAT"])

                for g_ in range(NG + 4):
                    if g_ < NG:
                        stageA(g_)
                    if 1 <= g_ <= NG:
                        stageB(g_ - 1)
                    if 2 <= g_ <= NG + 1:
                        stageB1b(g_ - 2)
                    if 3 <= g_ <= NG + 2:
                        stageB2(g_ - 3)
                    if g_ >= 4:
                        stageB2b(g_ - 4)
            A.release(mH)
            s.barrier()
            if DEBUG.get("ab_stop") == 2:
                A.release(m0)
                return False
            m3 = A.mark()
            wr = A.bf(8 * 1552).rearrange("p (k n) -> p k n", k=8)
            wo = A.bf(8 * 1024).rearrange("p (k n) -> p k n", k=8)
            wsb = A.bf(512).rearrange("p (g t) -> p g t", g=4)
            s.dma(wr, ABrest[j], r=["ABrest%d" % j], w=["wr"])
            s.dma(wo, ABout[j], r=["ABout%d" % j], w=["wo"])
            s.cp("dve", wsb, P("wsT%d" % j).rearrange("p (g t) -> p g t", g=4), r=["prm"], w=["wsb"])
            xts = [A.f32(4096).rearrange("p (c t) -> p c t", c=8)] * 2
            HAg = [A.bf(2048).rearrange("p (c t) -> p c t", c=4)] * 2
            HB = [A.bf(2048).rearrange("p (c t) -> p c t", c=4)] * 2
            U = [A.f32(2048).rearrange("p (c t) -> p c t", c=4)] * 2
            VBN = [A.bf(2048).rearrange("p (c t) -> p c t", c=4)] * 2
            gtmp = None
            sg = [A.f32(512) for _ in range(2)]
            vg_ = [A.f32(512) for _ in range(4)]
            t1 = [A.f32(512)] * 2
            s4 = [A.f32(64)]
            bsr = P("bs%d" % j).rearrange("p (g t) -> p g t", g=4)
            vgc = P("vg%d" % j)
            psA = Rot(PS[0:5]); psO = Rot(PS[5:8])
            n = 0
            for i in range(NT):
                b = 0
                ts_ = slice(i * TT, (i + 1) * TT)
                hnt = "HN%d" % i
                s.dma(xts[b], Xtile(X, i), r=["X%d" % i], w=["xt%d" % b])
                s4_ = s4[0]; s4t = "s40"
                for tc in range(4):
                    ch = i * 4 + tc
                    ps, pt = psA.next()
                    for kc in range(8):
                        s.mm(ps, HN[:, kc, ch * 128:(ch + 1) * 128], wr[:, kc, 1040:1552], start=(kc == 0), stop=(kc == 7),
                             r=["wr", hnt], w=[pt])
                    gelu(vg_[tc], ps, pt, gtmp, "vg%d" % tc)
                    for gg in range(4):
                        s.act(t1[0][:, 0:128], vg_[tc][:, gg * 128:(gg + 1) * 128], AF.Square, accum_out=s4_[:, tc * 4 + gg:tc * 4 + gg + 1],
                              r=["vg%d" % tc], w=["t10", s4t])
                s.ts("dve", s4_[:, 16:32], s4_[:, 0:16], 1.0 / 128, EPS, ALU.mult, ALU.add, r=[s4t], w=[s4t])
                s.act(s4_[:, 32:48], s4_[:, 16:32], AF.Sqrt, r=[s4t], w=[s4t])
                s.add("dve", lambda e, o=s4_[:, 48:64], i_=s4_[:, 32:48]: e.reciprocal(o, i_), r=[s4t], w=[s4t])
                for tc in range(4):
                    for gg in range(4):
                        s.ts("dve", VBN[b][:, tc, gg * 128:(gg + 1) * 128], vg_[tc][:, gg * 128:(gg + 1) * 128],
                             s4_[:, 48 + tc * 4 + gg:49 + tc * 4 + gg], None, ALU.mult, r=["vg%d" % tc, s4t], w=["VBN%d_%d" % (b, tc)])
                for mm_ in range(4):
                    ps, pt = psA.next()
                    for kc in range(8):
                        s.mm(ps, wr[:, kc, mm_ * 128:(mm_ + 1) * 128], HN[:, kc, ts_], start=(kc == 0), stop=(kc == 7),
                             r=["wr", hnt], w=[pt])
                    sb_ = n % 2; n += 1
                    s.act(sg[sb_], ps, AF.Sigmoid, r=[pt], w=["sg%d" % sb_])
                    s.tt("pool", HAg[b][:, mm_, :], HAT[:, mm_, ts_], sg[sb_], ALU.mult, r=["HAT", "sg%d" % sb_], w=["HAg%d" % b])
                for mm_ in range(4):
                    ps, pt = psA.next()
                    for kc in range(8):
                        s.mm(ps, wr[:, kc, 528 + mm_ * 128:528 + (mm_ + 1) * 128], HN[:, kc, ts_], start=(kc == 0), stop=(kc == 7),
                             r=["wr", hnt], w=[pt])
                    gelu(U[b][:, mm_, :], ps, pt, gtmp, "U%d_%d" % (b, mm_))
                for gg in range(4):
                    ps, pt = psA.next()
                    for tc in range(4):
                        s.mm(ps[:, tc * 128:(tc + 1) * 128], VBN[b][:, tc, gg * 128:(gg + 1) * 128], wsb[:, gg, :],
                             r=["VBN%d_%d" % (b, tc), "wsb"], w=[pt])
                    tb = 0
                    for tc in range(4):
                        s.stt("dve", t1[tb][:, tc * 128:(tc + 1) * 128], ps[:, tc * 128:(tc + 1) * 128], vgc[:, gg:gg + 1], bsr[:, gg, :],
                              ALU.mult, ALU.add, r=[pt, "prm"], w=["t1%d" % tb])
                    s.tt("dve", HB[b][:, gg, :], t1[tb], U[b][:, gg, :], ALU.mult, r=["t1%d" % tb, "U%d_%d" % (b, gg)], w=["HB%d" % b])
                for jj in range(8):
                    ps, pt = psO.next()
                    for kc in range(8):
                        rhs = HAg[b][:, kc, :] if kc < 4 else HB[b][:, kc - 4, :]
                        s.mm(ps, wo[:, kc, jj * 128:(jj + 1) * 128], rhs, start=(kc == 0), stop=(kc == 7),
                             r=["wo", "HAg%d" % b, "HB%d" % b], w=[pt])
                    s.tt("dve", xts[b][:, jj, :], ps, xts[b][:, jj, :], ALU.add, r=[pt, "xt%d" % b], w=["xo%d_%d" % (b, jj)])
                s.dma(Xtile(Y, i), xts[b], r=["xo%d_%d" % (b, jj) for jj in range(8)] + ["xt%d" % b], w=["X%d" % i, "xt%d" % b])
            A.release(m3)
            A.release(m0)
            s.barrier()

        def blk_mla(j, X, Y):
            m0 = A.mark()
            CQN = A.bf(3 * S).rearrange("p (c t) -> p c t", c=3)
            CKVN = A.bf(2 * S).rearrange("p (c t) -> p c t", c=2)
            KE = [A.bf(S) for _ in range(2)]
            TQ = A.f32(S)
            WQX = A.bf(3 * 16 * 128).rearrange("p (k h n) -> p k h n", k=3, h=16)
            WUKV = A.bf(2 * 2048).rearrange("p (k n) -> p k n", k=2)
            m1 = A.mark()
            TK = A.f32(S)
            wcin = A.bf(8 * 704).rearrange("p (k n) -> p k n", k=8)
            mt = A.mark()
            wuq = A.bf(3 * 1536).rearrange("p (k n) -> p k n", k=3)
            s.dma(wcin[:, :, 0:672], Ccin[j], r=["Ccin%d" % j], w=["wcin"])
            s.ts("dve", wcin[:, :, 672:688], wcin[:, :, 656:672], -1.0, None, ALU.mult, r=["wcin"], w=["wcin"])
            s.cp("dve", wcin[:, :, 688:704], wcin[:, :, 640:656], r=["wcin"], w=["wcin"])
            s.dma(wuq, Cuq[j], r=["Cuq%d" % j], w=["wuq"])
            wuq4 = wuq.rearrange("p k (h n) -> p k h n", h=16)
            for kc in range(3):
                s.cp("act", WQX[:, kc, :, 0:96], wuq4[:, kc], r=["wuq"], w=["WQX"])
                s.ts("dve", WQX[:, kc, :, 96:112], wuq4[:, kc, :, 80:96], -1.0, None, ALU.mult, r=["wuq"], w=["WQX"])
                s.cp("dve", WQX[:, kc, :, 112:128], wuq4[:, kc, :, 64:80], r=["wuq"], w=["WQX"])
            s.dma(WUKV, Cukv[j], r=["Cukv%d" % j], w=["WUKV"])
            pi = A.i32(S)
            pf = A.f32(S)
            rr = A.f32(S)
            s.dma(pi, posb, w=["pi"])
            s.cp("dve", pf, pi, r=["pi"], w=["pf"])
            for (tab, np_, fcol, pcol, tname) in ((TQ, 128, C("fq"), C("ph"), "TQ"), (TK, 64, C("fk"), C("pk"), "TK")):
                s.ts("dve", rr[0:np_], pf[0:np_], fcol[0:np_], pcol[0:np_], ALU.mult, ALU.add, r=["pf", "cst", "pi"], w=["rr"])
                s.cp("dve", pi[0:np_], rr[0:np_], r=["rr"], w=["pi"])
                s.cp("dve", tab[0:np_], pi[0:np_], r=["pi"], w=[tname])
                s.tt("dve", rr[0:np_], rr[0:np_], tab[0:np_], ALU.subtract, r=["rr", tname], w=["rr"])
                s.act(tab[0:np_], rr[0:np_], AF.Sin, scale=TWO_PI, r=["rr"], w=[tname])
            s.ts("dve", TQ, TQ, ATT_SCALE, None, ALU.mult, r=["TQ"], w=["TQ"])
            A.release(mt)
            s.barrier()
            if DEBUG.get("mla_stop") == 0:
                A.release(m0)
                return False
            xts = [A.f32(4096).rearrange("p (c t) -> p c t", c=8)] * 2
            hns = [A.bf(4096).rearrange("p (c t) -> p c t", c=8) for _ in range(2)]
            sq = A.bf(4096).rearrange("p (c t) -> p c t", c=8)
            rs = A.f32(512)
            CQ = A.f32(5 * 512).rearrange("p (c t) -> p c t", c=5)
            sq2 = A.bf(5 * 512).rearrange("p (c t) -> p c t", c=5)
            rs2 = [A.f32(512) for _ in range(2)]
            tk = A.f32(512)
            g = P("cn%d" % j)
            s.memset("pool", tk, 0.0, w=["tk0"])
            psA = Rot(PS[1:6])
            for i in range(NT):
                b = i % 2
                ts_ = slice(i * TT, (i + 1) * TT)
                hn = hns[b]
                if i == 0:
                    norm_tile(X, 0, g, xts[0], "xt0", hns[0], "hn0", sq, rs)
                for mm_ in range(5):
                    ps, pt = psA.next()
                    for kc in range(8):
                        s.mm(ps, wcin[:, kc, mm_ * 128:(mm_ + 1) * 128], hn[:, kc, :], start=(kc == 0), stop=(kc == 7),
                             r=["wcin", "hn%d" % b], w=[pt])
                    s.act(sq2[:, mm_, :], ps, AF.Square, r=[pt], w=["sq2_%d" % mm_])
                    s.cp("dve", CQ[:, mm_, :], ps, r=[pt], w=["CQ%d" % mm_])
                for (lo, hi, dim, gcol, dst, dn, ri) in ((0, 3, 384.0, P("qg%d" % j), CQN, "CQN", 0), (3, 5, 256.0, P("kvg%d" % j), CKVN, "CKVN", 1)):
                    ps, pt = psA.next()
                    for mm_ in range(lo, hi):
                        s.mm(ps, ones_bf, sq2[:, mm_, :], start=(mm_ == lo), stop=(mm_ == hi - 1), r=["sq2_%d" % mm_, "cstb"], w=[pt])
                    rt = "rs2%d" % ri
                    s.ts("dve", rs2[ri], ps, 1.0 / dim, EPS, ALU.mult, ALU.add, r=[pt], w=[rt])
                    s.act(rs2[ri], rs2[ri], AF.Ln, r=[rt], w=[rt])
                    s.act(rs2[ri], rs2[ri], AF.Exp, scale=-0.5, r=[rt], w=[rt])
                    for mm_ in range(lo, hi):
                        s.stt("dve", dst[:, mm_ - lo, ts_], CQ[:, mm_, :], gcol[:, mm_ - lo:mm_ - lo + 1], rs2[ri], ALU.mult, ALU.mult,
                              r=["CQ%d" % mm_, rt, "prm"], w=["%s%d" % (dn, i)])
                ps, pt = psA.next()
                for kc in range(8):
                    s.mm(ps[0:64, :], wcin[:, kc, 640:704], hn[:, kc, :], start=(kc == 0), stop=(kc == 7), r=["wcin", "hn%d" % b], w=[pt])
                if i + 1 < NT:
                    norm_tile(X, i + 1, g, xts[0], "xt0", hns[1 - b], "hn%d" % (1 - b), sq, rs)
                s.tt("dve", tk[0:64], ps[0:64, :], TK[0:64, ts_], ALU.mult, r=[pt, "TK", "tk0"], w=["tk"])
                ps2, pt2 = psA.next()
                s.mm(ps2, C("sel"), tk, r=["cst", "tk", "tk0"], w=[pt2])
                s.cp("act", KE[0][64:128, ts_], ps2[64:128, :], r=[pt2], w=["KEr0"])
                s.cp("dve", KE[1][64:128, ts_], ps2[64:128, :], r=[pt2], w=["KEr1"])
            A.release(m1)
            s.barrier()
            if DEBUG.get("mla_stop") == 1:
                A.release(m0)
                return False
            m2 = A.mark()
            OT = A.bf(8 * S).rearrange("p (c t) -> p c t", c=8)
            m2b = A.mark()
            VX = [A.bf(32 * 128).rearrange("p (c n) -> p c n", n=128) for _ in range(2)]
            QX = [A.bf(512) for _ in range(2)]
            PT = [A.bf(512) for _ in range(4)]
            rden = [A.f32(512) for _ in range(2)]
            for b in range(2):
                s.memset("pool", VX[b][:, :, 64:128], 1.0, w=["VXo%d" % b])
            psS = Rot(PS[0:3]); psO = Rot(PS[3:5]); psQ = Rot(PS[5:6]); psK = Rot(PS[6:8])
            NH = DEBUG.get('mla_heads', 16)
            LA = 2

            def kv_prep(h):
                hb = h % 2
                for i in range(NT):
                    ts_ = slice(i * TT, (i + 1) * TT)
                    ps, pt = psK.next()
                    for kc in range(2):
                        s.mm(ps[0:64, :], WUKV[:, kc, h * 128:h * 128 + 64], CKVN[:, kc, ts_], start=(kc == 0), stop=(kc == 1),
                             r=["WUKV", "CKVN%d" % i], w=[pt])
                    s.cp("dve", KE[hb][0:64, ts_], ps[0:64, :], r=[pt], w=["KE%d_%d" % (hb, i)])
                for c8 in range(4):
                    ps, pt = psK.next()
                    for cc in range(8):
                        ch = c8 * 8 + cc
                        for kc in range(2):
                            s.mm(ps[:, cc * 64:(cc + 1) * 64], CKVN[:, kc, ch * 128:(ch + 1) * 128], WUKV[:, kc, h * 128 + 64:h * 128 + 128],
                                 start=(kc == 0), stop=(kc == 1), r=["WUKV", "CKVN%d" % (ch // 4)], w=[pt])
                    s.cp("dve", VX[hb][:, c8 * 8:(c8 + 1) * 8, 0:64], ps.rearrange("p (a n) -> p a n", n=64),
                         r=[pt, "VXo%d" % hb], w=["VX%d_%d" % (hb, c8)])

            def q_prep(h, qt):
                ts_ = slice(qt * TT, (qt + 1) * TT)
                pq, pqt = psQ.next()
                for kc in range(3):
                    s.mm(pq, WQX[:, kc, h, :], CQN[:, kc, ts_], start=(kc == 0), stop=(kc == 2), r=["WQX", "CQN%d" % qt], w=[pqt])
                qb = qt % 2
                s.tt("dve", QX[qb], pq, TQ[:, ts_], ALU.mult, r=[pqt, "TQ"], w=["QX%d" % qb])

            npt = 0
            kv_prep(0)
            for h in range(NH):
                hb = h % 2
                ket = ["KE%d_%d" % (hb, i) for i in range(NT)] + ["KEr%d" % hb]
                vxt = ["VX%d_%d" % (hb, c8) for c8 in range(4)] + ["VXo%d" % hb]
                seq = [(qt, kc) for qt in range(NT) for kc in range(32)]
                q_prep(h, 0)
                pos_ = {}
                ptb = {}
                for n in range(len(seq) + LA):
                    if n < len(seq):
                        qt, kc = seq[n]
                        if kc == 0:
                            pos_[qt] = psO.next()
                        if kc == 12 and qt + 1 < NT:
                            q_prep(h, qt + 1)
                        if qt == 4 and kc == 20 and h + 1 < NH:
                            kv_prep(h + 1)
                        qb = qt % 2
                        ps, pt = psS.next()
                        s.mm(ps, KE[hb][:, kc * 128:(kc + 1) * 128], QX[qb], r=ket + ["QX%d" % qb], w=[pt])
                        pb = npt % 4; npt += 1
                        ptb[n] = pb
                        s.act(PT[pb], ps, AF.Exp, r=[pt], w=["PT%d" % pb])
                    if n >= LA:
                        qt2, kc2 = seq[n - LA]
                        po, pot = pos_[qt2]
                        pb = ptb.pop(n - LA)
                        s.mm(po, VX[hb][:, kc2, :], PT[pb], start=(kc2 == 0), stop=(kc2 == 31), r=vxt + ["PT%d" % pb], w=[pot])
                        if kc2 == 31:
                            ts2 = slice(qt2 * TT, (qt2 + 1) * TT)
                            rb = qt2 % 2
                            s.add("dve", lambda e, o=rden[rb][64:128], i_=po[64:128, :]: e.reciprocal(o, i_), r=[pot], w=["rden%d" % rb])
                            r0 = (h % 2) * 64
                            s.tt("dve", OT[r0:r0 + 64, h // 2, ts2], po[0:64, :], rden[rb][64:128], ALU.mult,
                                 r=[pot, "rden%d" % rb], w=["OT%d" % qt2])
            A.release(m2b)
            s.barrier()
            A.off = m0
            wo = A.bf(8 * 1024).rearrange("p (k n) -> p k n", k=8)
            xts = [A.f32(4096).rearrange("p (c t) -> p c t", c=8) for _ in range(2)]
            s.dma(wo, Cout[j], r=["Cout%d" % j], w=["wo"])
            psO = Rot(PS)
            load_x(X, 0, xts[0], "xt0")
            for i in range(NT):
                b = i % 2
                ts_ = slice(i * TT, (i + 1) * TT)
                if i + 1 < NT:
                    load_x(X, i + 1, xts[1 - b], "xt%d" % (1 - b))
                for jj in range(8):
                    ps, pt = psO.next()
                    for kc in range(8):
                        s.mm(ps, wo[:, kc, jj * 128:(jj + 1) * 128], OT[:, kc, ts_], start=(kc == 0), stop=(kc == 7), r=["wo", "OT%d" % i], w=[pt])
                    s.tt("dve", xts[b][:, jj, :], ps, xts[b][:, jj, :], ALU.add, r=[pt, "xt%d" % b], w=["xo%d_%d" % (b, jj)])
                s.dma(Xtile(Y, i), xts[b], r=["xo%d_%d" % (b, jj) for jj in range(8)] + ["xt%d" % b], w=["X%d" % i, "xt%d" % b])
            A.release(m0)
            s.barrier()

        cur = 0
        body = [b for b in blocks if b not in ("load", "final")]
        if body:
            conv_for(body[0])
        Xcur = x_in
        for bi, b in enumerate(body):
            if bi + 1 < len(body):
                conv_for(body[bi + 1])
            X, Y = Xcur, XS[cur]
            if b.startswith("ffn"):
                blk_ffn(int(b[3:]), X, Y)
            elif b.startswith("ab"):
                if blk_ab(int(b[2:]), X, Y) is False:
                    continue
            elif b.startswith("mla"):
                if blk_mla(int(b[3:]), X, Y) is False:
                    continue
            Xcur = Y
            cur = 1 - cur
        blk_final(Xcur)
        s.emit(nc)
        print("ops:", s.nops, "arena peak words:", A.peak)
    return nc


_W_NAMES = ["ab_w_in", "ab_w_out", "c_w_in", "c_w_uq", "c_w_ukv", "c_w_out", "ffn_w1", "ffn_w2"]


def run(inputs, blocks=None, n_cores=8, trace=False):
    nc = build(blocks)
    need = needed_weights(ALL_BLOCKS if blocks is None else blocks)
    prm, cst = host_pack(inputs)
    x = np.asarray(inputs["x"], np.float32)
    xT = np.ascontiguousarray(x.transpose(0, 2, 1)).reshape(x.shape[0], 8, 128, S)
    pos = np.asarray(inputs["positions"], np.int32)
    shared = {k: np.ascontiguousarray(np.asarray(inputs[k], np.float32)) for k in _W_NAMES if k in need}
    in_maps = []
    for c in range(n_cores):
        mp = {"xT": xT[c], "posb": np.ascontiguousarray(np.broadcast_to(pos[c][None, :], (128, S))),
              "prm": prm, "cst": cst}
        mp.update(shared)
        in_maps.append(mp)
    res = run_bass_kernel_spmd(nc, in_maps, core_ids=list(range(n_cores)), trace=trace)
    out = np.stack([np.asarray(r["outT"]).reshape(D, S).T for r in res.results], axis=0)
    return out, res


def kernel(**inputs):
    import os, json
    if os.environ.get("KBLOCKS") is not None:
        DEBUG.update(json.loads(os.environ.get("KDEBUG", "{}")))
        out, _ = run(inputs, blocks=["load"] + [b for b in os.environ["KBLOCKS"].split(",") if b] + ["final"])
        return out.astype(np.float32)
    out, _ = run(inputs)
    return out.astype(np.float32)
```

```python
import numpy as np
import concourse.bass as bass
import concourse.mybir as mybir

F32 = mybir.dt.float32
BF16 = mybir.dt.bfloat16
I32 = mybir.dt.int32
AF = mybir.ActivationFunctionType
ALU = mybir.AluOpType
AX = mybir.AxisListType


class Op:
    __slots__ = ("stream", "comp", "pos", "fn", "waits", "signal", "vc", "isdma", "sigval")


class Sch:
    STREAMS = ("pe", "act", "dve", "pool", "sp")

    def __init__(self, ndma=16):
        self.ops = {s: [] for s in self.STREAMS}
        self.comp_ops = {}
        self.last_w = {}
        self.readers = {}
        self.clock = {s: {} for s in self.STREAMS}
        self.ndma = {"sp": ndma, "pool": 8, "act": 4}
        self.dma_rr = {"sp": 0, "pool": 0, "act": 0}
        self.nops = 0

    def add(self, stream, fn, r=(), w=(), dma=False):
        op = Op()
        op.stream = stream
        op.fn = fn
        op.isdma = dma
        op.signal = dma
        op.waits = []
        deps = []
        if any(isinstance(t, str) and t.startswith("ps") and t[2:].isdigit() for t in r):
            w = list(w) + [t for t in r if isinstance(t, str) and t.startswith("ps") and t[2:].isdigit()]
        for t in r:
            x = self.last_w.get(t)
            if x is not None:
                deps.append((x, 0))
        for t in w:
            x = self.last_w.get(t)
            if x is not None:
                deps.append((x, 1))
            for x in self.readers.get(t, ()):
                deps.append((x, 2))
        if dma:
            k = self.dma_rr[stream]
            self.dma_rr[stream] = (k + 1) % self.ndma[stream]
            op.comp = "dma_%s%d" % (stream, k)
            lst = self.comp_ops.setdefault(op.comp, [])
            if lst:
                deps.append((lst[-1], 3))
        else:
            op.comp = stream
            lst = self.comp_ops.setdefault(op.comp, [])
        op.pos = len(lst)
        lst.append(op)
        clk = self.clock[stream]
        for x, kind in sorted(deps, key=lambda d: -d[0].pos):
            if x is op:
                continue
            if x.comp == stream and not x.isdma:
                if kind != 0 or stream == "pe":
                    continue
            if clk.get(x.comp, -1) >= x.pos:
                continue
            op.waits.append(x)
            x.signal = True
            for e, p in x.vc.items():
                if clk.get(e, -1) < p:
                    clk[e] = p
        op.vc = dict(clk)
        op.vc[op.comp] = op.pos
        for t in r:
            self.readers.setdefault(t, []).append(op)
        for t in w:
            self.last_w[t] = op
            self.readers[t] = []
        self.ops[stream].append(op)
        self.nops += 1
        return op

    def barrier(self):
        lasts = [lst[-1] for lst in self.comp_ops.values() if lst]
        for s in self.STREAMS:
            op = Op()
            op.stream = s
            op.fn = None
            op.isdma = False
            op.signal = False
            op.waits = []
            op.comp = s
            lst = self.comp_ops.setdefault(s, [])
            op.pos = len(lst)
            clk = self.clock[s]
            for x in sorted(lasts, key=lambda d: -d.pos):
                if clk.get(x.comp, -1) >= x.pos:
                    continue
                op.waits.append(x)
                x.signal = True
                for e, p in x.vc.items():
                    if clk.get(e, -1) < p:
                        clk[e] = p
            op.pos = len(lst) - 1
            op.vc = dict(clk)
            self.ops[s].append(op)
        self.last_w = {}
        self.readers = {}

    def emit(self, nc):
        import contextlib
        for comp, lst in self.comp_ops.items():
            n = 0
            for op in lst:
                if op.signal:
                    n += 16 if op.isdma else 1
                op.sigval = n
        comps = list(self.comp_ops.keys())
        with contextlib.ExitStack() as es:
            sems = {c: es.enter_context(nc.semaphore("s_" + c)) for c in comps}
            block = es.enter_context(nc.Block())

            def run(eng, ops):
                for op in ops:
                    for x in op.waits:
                        eng.wait_ge(sems[x.comp], x.sigval)
                    if op.fn is None:
                        continue
                    ins = op.fn(eng)
                    if op.signal:
                        ins.then_inc(sems[op.comp], 16 if op.isdma else 1)

            @block.tensor
            def _(e):
                run(e, self.ops["pe"])

            @block.scalar
            def _(e):
                run(e, self.ops["act"])

            @block.vector
            def _(e):
                run(e, self.ops["dve"])

            @block.gpsimd
            def _(e):
                run(e, self.ops["pool"])

            @block.sync
            def _(e):
                run(e, self.ops["sp"])

    def mm(self, out, lhsT, rhs, start=True, stop=True, r=(), w=(), **kw):
        return self.add("pe", lambda e: e.matmul(out, lhsT, rhs, start=start, stop=stop, **kw), r, w)

    def tr(self, out, in_, ident, r=(), w=()):
        return self.add("pe", lambda e: e.transpose(out, in_, ident), r, w)

    def act(self, out, in_, func, bias=None, scale=None, accum_out=None, r=(), w=(), eng="act"):
        kw = {}
        if bias is not None:
            kw["bias"] = bias
        if scale is not None:
            kw["scale"] = scale
        if accum_out is not None:
            kw["accum_out"] = accum_out
        return self.add(eng, lambda e: e.activation(out, in_, func, **kw), r, w)

    def tt(self, eng, out, in0, in1, op, r=(), w=()):
        return self.add(eng, lambda e: e.tensor_tensor(out, in0, in1, op), r, w)

    def ts(self, eng, out, in0, s1, s2, op0, op1=None, r=(), w=(), accum_out=None):
        if op1 is None:
            return self.add(eng, lambda e: e.tensor_scalar(out, in0, s1, None, op0), r, w)
        if accum_out is not None:
            return self.add(eng, lambda e: e.tensor_scalar(out, in0, s1, s2, op0, op1, accum_out=accum_out), r, w)
        return self.add(eng, lambda e: e.tensor_scalar(out, in0, s1, s2, op0, op1), r, w)

    def stt(self, eng, out, in0, scalar, in1, op0, op1, r=(), w=()):
        return self.add(eng, lambda e: e.scalar_tensor_tensor(out, in0, scalar, in1, op0, op1), r, w)

    def cp(self, eng, out, in_, r=(), w=()):
        if eng == "act":
            return self.add(eng, lambda e: e.copy(out, in_), r, w)
        return self.add(eng, lambda e: e.tensor_copy(out, in_), r, w)

    def memset(self, eng, ap, val, w=()):
        return self.add(eng, lambda e: e.memset(ap, val), (), w)

    def dma(self, out, in_, r=(), w=(), q="sp", **kw):
        return self.add(q, lambda e: e.dma_start(out=out, in_=in_, **kw), r, w, dma=True)

import contextlib
from concourse.bass_utils import run_bass_kernel_spmd

S = 4096
D = 1024
TT = 512
NT = 8
EPS = 1e-6
ATT_SCALE = 96 ** -0.5
QSCALE = 128 ** -0.5
TWO_PI = 6.283185307179586


def _layout(items):
    off = {}
    o = 0
    for name, n in items:
        off[name] = (o, n)
        o += n
    return off, o


PRM_ITEMS = []
for _j in range(2):
    PRM_ITEMS += [("abn%d" % _j, 8), ("cn%d" % _j, 8), ("conv%d" % _j, 40), ("gb%d" % _j, 64),
                  ("hgc%d" % _j, 4), ("vg%d" % _j, 4), ("wsT%d" % _j, 512), ("bs%d" % _j, 512),
                  ("qg%d" % _j, 3), ("kvg%d" % _j, 2)]
for _l in range(4):
    PRM_ITEMS += [("fn%d" % _l, 8)]
PRM_ITEMS += [("finn", 8)]
PRM_OFF, NPRM = _layout(PRM_ITEMS)
CST_ITEMS = [("ident", 128), ("maskF", 128), ("maskB", 128), ("ones", 128), ("sel", 128),
             ("fq", 1), ("ph", 1), ("fk", 1), ("pk", 1)]
CST_OFF, NCST = _layout(CST_ITEMS)


def host_pack(inp):
    prm = np.zeros((128, NPRM), np.float32)

    def put(name, arr):
        o, n = PRM_OFF[name]
        prm[:, o:o + n] = np.asarray(arr, np.float32).reshape(128, n)

    def colmaj(v, nch):
        return np.asarray(v).reshape(nch, 128).T

    for j in range(2):
        put("abn%d" % j, colmaj(inp["ab_norm"][j], 8))
        put("cn%d" % j, colmaj(inp["c_norm"][j], 8))
        cv = np.asarray(inp["ab_conv"][j])
        put("conv%d" % j, cv.T.reshape(8, 128, 5).transpose(1, 0, 2).reshape(128, 40))
        put("gb%d" % j, np.tile(np.asarray(inp["ab_gate_b"][j])[None, :], (128, 4)))
        put("hgc%d" % j, colmaj(inp["ab_head_g"][j], 4))
        put("vg%d" % j, colmaj(inp["ab_v_g"][j], 4))
        ws = np.asarray(inp["ab_ws"][j])
        put("wsT%d" % j, ws.transpose(2, 0, 1).reshape(128, 512))
        bs = np.asarray(inp["ab_bs"][j])
        put("bs%d" % j, np.tile(bs.reshape(1, 512), (128, 1)))
        put("qg%d" % j, colmaj(inp["c_q_g"][j], 3))
        put("kvg%d" % j, colmaj(inp["c_kv_g"][j], 2))
    for l in range(4):
        put("fn%d" % l, colmaj(inp["ffn_norm"][l], 8))
    put("finn", colmaj(inp["final_norm"], 8))

    cst = np.zeros((128, NCST), np.float32)

    def putc(name, arr):
        o, n = CST_OFF[name]
        cst[:, o:o + n] = np.asarray(arr, np.float32).reshape(128, n)

    putc("ident", np.eye(128))
    ii = np.arange(128)
    putc("maskF", (ii[:, None] <= ii[None, :]).astype(np.float32))
    putc("maskB", (ii[:, None] >= ii[None, :]).astype(np.float32))
    putc("ones", np.ones((128, 128)))
    sel = np.zeros((128, 128), np.float32)
    for jj in range(32):
        sel[jj, 64 + jj] = 1; sel[jj, 96 + jj] = 1
        sel[32 + jj, 64 + jj] = 1; sel[32 + jj, 96 + jj] = 1
    putc("sel", sel)
    half = 16
    freq = (10000.0 ** (-np.arange(half, dtype=np.float32) / half)).astype(np.float32)
    f2 = (freq.astype(np.float64) / TWO_PI).astype(np.float32)
    fq = np.zeros(128, np.float32); ph = np.zeros(128, np.float32)
    ph[0:64] = 0.25
    for p in range(64, 96):
        fq[p] = f2[(p - 64) % 16]; ph[p] = 0.25
    for p in range(96, 128):
        fq[p] = f2[(p - 96) % 16]; ph[p] = 0.0
    fk = np.zeros(128, np.float32); pk = np.zeros(128, np.float32)
    for p in range(0, 32):
        fk[p] = f2[p % 16]; pk[p] = 0.25
    for p in range(32, 64):
        fk[p] = f2[p % 16]; pk[p] = 0.0
    putc("fq", fq); putc("ph", ph); putc("fk", fk); putc("pk", pk)
    return prm, cst


class Arena:
    def __init__(self, ap):
        self.ap = ap
        self.off = 0
        self.cap = ap.shape[1]
        self.peak = 0

    def _take(self, w):
        w = (w + 7) // 8 * 8
        o = self.off
        self.off += w
        self.peak = max(self.peak, self.off)
        assert self.off <= self.cap, ("arena overflow", self.off, self.cap)
        return o, w

    def f32(self, n):
        o, w = self._take(n)
        return self.ap[:, o:o + n]

    def bf(self, n):
        o, w = self._take((n + 1) // 2)
        return self.ap[:, o:o + w].bitcast(BF16)[:, 0:n]

    def i32(self, n):
        o, w = self._take(n)
        return self.ap[:, o:o + n].bitcast(I32)

    def mark(self):
        return self.off

    def release(self, m):
        self.off = m


class Rot:
    def __init__(self, items):
        self.items = items
        self.i = 0

    def next(self):
        it = self.items[self.i % len(self.items)]
        self.i += 1
        return it


def needed_weights(blocks):
    need = set()
    for b in blocks:
        if b.startswith("ffn"):
            need |= {"ffn_w1", "ffn_w2"}
        elif b.startswith("ab"):
            need |= {"ab_w_in", "ab_w_out"}
        elif b.startswith("mla"):
            need |= {"c_w_in", "c_w_uq", "c_w_ukv", "c_w_out"}
    return need


DEBUG = {}


ALL_BLOCKS = ["load", "ab0", "ffn0", "mla0", "ffn1", "ab1", "ffn2", "mla1", "ffn3", "final"]


def build(blocks=None, debug_out=None):
    blocks = list(ALL_BLOCKS if blocks is None else blocks)
    nc = bass.Bass("TRN2", target_bir_lowering=False)
    s = Sch()

    def din(name, shape, dt=F32):
        return nc.dram_tensor(name, list(shape), dt, kind="ExternalInput").ap()

    x_in = din("xT", [8, 128, S])
    posb = din("posb", [128, S], I32)
    prm_d = din("prm", [128, NPRM])
    cst_d = din("cst", [128, NCST])
    need = needed_weights(blocks)
    dw = lambda name, shape: din(name, shape) if name in need else None
    ab_w_in = dw("ab_w_in", [2, 1024, 3088])
    ab_w_out = dw("ab_w_out", [2, 1024, 1024])
    c_w_in = dw("c_w_in", [2, 1024, 672])
    c_w_uq = dw("c_w_uq", [2, 384, 1536])
    c_w_ukv = dw("c_w_ukv", [2, 256, 2048])
    c_w_out = dw("c_w_out", [2, 1024, 1024])
    ffn_w1 = dw("ffn_w1", [4, 1024, 4096])
    ffn_w2 = dw("ffn_w2", [4, 4096, 1024])
    out_d = nc.dram_tensor("outT", [8, 128, S], F32, kind="ExternalOutput").ap()

    def scratch(name, shape, dt=BF16):
        return nc.dram_tensor(name, list(shape), dt).ap()

    XS = [scratch("xs0", [8, 128, S], F32), scratch("xs1", [8, 128, S], F32)]
    W1s = [scratch("w1s%d" % l, [8, 128, 8, 512]) for l in range(4)]
    W2s = [scratch("w2s%d" % l, [4, 128, 32, 256]) for l in range(4)]
    ABqkv = [scratch("abqkv%d" % j, [4, 128, 8, 3, 128]) for j in range(2)]
    ABrest = [scratch("abrest%d" % j, [128, 8, 1552]) for j in range(2)]
    ABout = [scratch("about%d" % j, [128, 8, 1024]) for j in range(2)]
    Ccin = [scratch("ccin%d" % j, [128, 8, 672]) for j in range(2)]
    Cuq = [scratch("cuq%d" % j, [128, 3, 1536]) for j in range(2)]
    Cukv = [scratch("cukv%d" % j, [128, 2, 2048]) for j in range(2)]
    Cout = [scratch("cout%d" % j, [128, 8, 1024]) for j in range(2)]

    es = contextlib.ExitStack()
    with es:
        arena_t = es.enter_context(nc.sbuf_tensor("arena", [128, 49152], F32))
        prm = es.enter_context(nc.sbuf_tensor("prm_sb", [128, NPRM], F32))
        cst = es.enter_context(nc.sbuf_tensor("cst_sb", [128, NCST], F32))
        cstb = es.enter_context(nc.sbuf_tensor("cstb_sb", [128, 256], BF16))
        PSt = [es.enter_context(nc.psum_tensor("ps%d" % k, [128, 512], F32)) for k in range(8)]
        PS = [(PSt[k][:], "ps%d" % k) for k in range(8)]
        A = Arena(arena_t[:])

        def P(name, a=None, b=None):
            o, n = PRM_OFF[name]
            a = 0 if a is None else a
            b = n if b is None else b
            return prm[:, o + a:o + b]

        def C(name, a=None, b=None):
            o, n = CST_OFF[name]
            a = 0 if a is None else a
            b = n if b is None else b
            return cst[:, o + a:o + b]

        ident = C("ident")
        ident_bf = cstb[:, 0:128]
        ones_bf = cstb[:, 128:256]

        s.dma(prm[:], prm_d, w=["prm"])
        s.dma(cst[:], cst_d, w=["cst"])
        s.cp("dve", ident_bf, C("ident"), r=["cst"], w=["cstb"])
        s.cp("dve", ones_bf, C("ones"), r=["cst"], w=["cstb"])

        def conv_ffn(l):
            for pc in range(8):
                s.dma(W1s[l][pc], ffn_w1[l][:, pc * 512:(pc + 1) * 512].rearrange("(kc p) n -> p kc n", p=128),
                      w=["W1s%d_%d" % (l, pc)], q="pool")
            for pc in range(4):
                s.dma(W2s[l][pc], ffn_w2[l][:, pc * 256:(pc + 1) * 256].rearrange("(kc p) n -> p kc n", p=128),
                      w=["W2s%d_%d" % (l, pc)], q="pool")

        def conv_ab(j):
            for h in range(4):
                for t in range(3):
                    s.dma(ABqkv[j][h][:, :, t, :],
                          ab_w_in[j][:, t * 512 + h * 128: t * 512 + (h + 1) * 128].rearrange("(kc p) n -> p kc n", p=128),
                          w=["ABqkv%d_%d_%d" % (j, h, t)], q="pool")
            s.dma(ABrest[j], ab_w_in[j][:, 1536:3088].rearrange("(kc p) n -> p kc n", p=128), w=["ABrest%d" % j], q="pool")
            s.dma(ABout[j], ab_w_out[j].rearrange("(kc p) n -> p kc n", p=128), w=["ABout%d" % j], q="pool")

        def conv_mla(j):
            s.dma(Ccin[j], c_w_in[j].rearrange("(kc p) n -> p kc n", p=128), w=["Ccin%d" % j], q="pool")
            s.dma(Cuq[j], c_w_uq[j].rearrange("(kc p) n -> p kc n", p=128), w=["Cuq%d" % j], q="pool")
            s.dma(Cukv[j], c_w_ukv[j].rearrange("(kc p) n -> p kc n", p=128), w=["Cukv%d" % j], q="pool")
            s.dma(Cout[j], c_w_out[j].rearrange("(kc p) n -> p kc n", p=128), w=["Cout%d" % j], q="pool")

        def conv_for(b):
            if b.startswith("ffn"):
                conv_ffn(int(b[3:]))
            elif b.startswith("ab"):
                conv_ab(int(b[2:]))
            elif b.startswith("mla"):
                conv_mla(int(b[3:]))

        def Xtile(X, i):
            return X[:, :, i * TT:(i + 1) * TT].rearrange("c p t -> p c t")

        def blk_load(Y):
            m = A.mark()
            xin = [A.f32(4096) for _ in range(2)]
            st = [A.f32(4096) for _ in range(2)]
            psr = Rot(PS)
            for i in range(NT):
                b = i % 2
                xi = xin[b].rearrange("p (a n) -> p a n", a=4)
                s.dma(xi, x_in[i * 512:(i + 1) * 512, :].rearrange("(a p) n -> p a n", p=128), w=["xin%d" % b])
                sv = st[b].rearrange("p (c t) -> p c t", c=8)
                for c in range(8):
                    ps, pt = psr.next()
                    for a in range(4):
                        s.tr(ps[:, a * 128:(a + 1) * 128], xi[:, a, c * 128:(c + 1) * 128], ident,
                             r=["xin%d" % b, "cst"], w=[pt])
                    s.cp("dve" if c % 2 == 0 else "act", sv[:, c, :], ps, r=[pt], w=["st%d_%d" % (b, c)])
                s.dma(Xtile(Y, i), sv, r=["st%d_%d" % (b, c) for c in range(8)], w=["X%d" % i])
            A.release(m)
            s.barrier()

        def load_x(X, i, xt, xt_tok):
            s.dma(xt, Xtile(X, i), r=["X%d" % i], w=[xt_tok])

        def norm_tile(X, i, g, xt, xt_tok, hn, hn_tok, sq, rs, preloaded=False):
            if not preloaded:
                load_x(X, i, xt, xt_tok)
            for c in range(8):
                s.act(sq[:, c, :], xt[:, c, :], AF.Square, r=[xt_tok], w=["sq%d" % c])
            ps, pt = PS[0]
            for c in range(8):
                s.mm(ps, ones_bf, sq[:, c, :], start=(c == 0), stop=(c == 7), r=["sq%d" % c, "cstb"], w=[pt])
            s.ts("dve", rs, ps, 1.0 / D, EPS, ALU.mult, ALU.add, r=[pt], w=["rs"])
            s.act(rs, rs, AF.Ln, r=["rs"], w=["rs"])
            s.act(rs, rs, AF.Exp, scale=-0.5, r=["rs"], w=["rs"])
            for c in range(8):
                s.stt("dve", hn[:, c, :], xt[:, c, :], g[:, c:c + 1], rs, ALU.mult, ALU.mult,
                      r=[xt_tok, "rs", "prm"], w=[hn_tok])

        def blk_ffn(l, X, Y):
            m = A.mark()
            xts = [A.f32(4096).rearrange("p (c t) -> p c t", c=8) for _ in range(2)]
            hns = [A.bf(4096).rearrange("p (c t) -> p c t", c=8) for _ in range(2)]
            sq = A.bf(4096).rearrange("p (c t) -> p c t", c=8)
            rs = A.f32(512)
            hb = A.bf(32 * 512).rearrange("p (c t) -> p c t", c=32)
            NW1, NW2 = 4, 3
            w1b = [A.bf(8 * 512).rearrange("p (k n) -> p k n", k=8) for _ in range(NW1)]
            w2b = [A.bf(32 * 256).rearrange("p (k n) -> p k n", k=32) for _ in range(NW2)]
            r32 = [A.f32(512) for _ in range(2)]
            g = P("fn%d" % l)
            ps1 = Rot(PS[1:4])
            ps2 = Rot(PS[4:8])
            n1 = 0
            n2 = 0
            nr = 0
            norm_tile(X, 0, g, xts[0], "xt0", hns[0], "hn0", sq, rs)
            for i in range(NT):
                b = i % 2
                xt, hn = xts[b], hns[b]
                for pc in range(8):
                    wb = n1 % NW1
                    n1 += 1
                    s.dma(w1b[wb], W1s[l][pc], r=["W1s%d_%d" % (l, pc)], w=["w1b%d" % wb])
                    for mm_ in range(4):
                        ps, pt = ps1.next()
                        for kc in range(8):
                            s.mm(ps, w1b[wb][:, kc, mm_ * 128:(mm_ + 1) * 128], hn[:, kc, :],
                                 start=(kc == 0), stop=(kc == 7), r=["w1b%d" % wb, "hn%d" % b], w=[pt])
                        rb = nr % 2
                        nr += 1
                        s.act(r32[rb], ps, AF.Relu, r=[pt], w=["r32%d" % rb])
                        s.tt("dve", hb[:, pc * 4 + mm_, :], r32[rb], r32[rb], ALU.mult,
                             r=["r32%d" % rb], w=["hb%d" % (pc * 4 + mm_)])
                    if pc == 3 and i + 1 < NT:
                        nb_ = (i + 1) % 2
                        norm_tile(X, i + 1, g, xts[nb_], "xt%d" % nb_, hns[nb_], "hn%d" % nb_, sq, rs)
                for pc in range(4):
                    wb = n2 % NW2
                    n2 += 1
                    s.dma(w2b[wb], W2s[l][pc], r=["W2s%d_%d" % (l, pc)], w=["w2b%d" % wb])
                    for jj in range(2):
                        j = pc * 2 + jj
                        ps, pt = ps2.next()
                        for kc in range(32):
                            s.mm(ps, w2b[wb][:, kc, jj * 128:(jj + 1) * 128], hb[:, kc, :],
                                 start=(kc == 0), stop=(kc == 31), r=["w2b%d" % wb, "hb%d" % kc], w=[pt])
                        s.tt("dve", xt[:, j, :], ps, xt[:, j, :], ALU.add, r=[pt, "xt%d" % b], w=["xo%d_%d" % (b, j)])
                s.dma(Xtile(Y, i), xt, r=["xo%d_%d" % (b, j) for j in range(8)] + ["xt%d" % b],
                      w=["X%d" % i, "xt%d" % b])
            A.release(m)
            s.barrier()

        def blk_final(X):
            m = A.mark()
            xts = [A.f32(4096).rearrange("p (c t) -> p c t", c=8) for _ in range(2)]
            hns = [A.f32(4096).rearrange("p (c t) -> p c t", c=8) for _ in range(2)]
            sq = A.bf(4096).rearrange("p (c t) -> p c t", c=8)
            rs = A.f32(512)
            g = P("finn")
            load_x(X, 0, xts[0], "xt0")
            for i in range(NT):
                b = i % 2
                xt, hn = xts[b], hns[b]
                if i + 1 < NT:
                    load_x(X, i + 1, xts[1 - b], "xt%d" % (1 - b))
                norm_tile(X, i, g, xt, "xt%d" % b, hn, "hn%d" % b, sq, rs, preloaded=True)
                s.dma(Xtile(out_d, i), hn, r=["hn%d" % b], w=["out%d" % i, "hn%d" % b])
            s.add("sp", None, r=["out%d" % i for i in range(NT)])
            A.release(m)

        def gelu(out, ps, pt, gt, otok):
            s.act(out, ps, AF.Gelu_apprx_tanh, r=[pt], w=[otok])

        def blk_ab(j, X, Y):
            m0 = A.mark()
            if DEBUG.get("ab_stop") == 0:
                s.barrier()
                return False
            HN = A.bf(8 * S).rearrange("p (c t) -> p c t", c=8)
            HAT = A.bf(4 * S).rearrange("p (c t) -> p c t", c=4)
            mH = A.mark()
            G = A.f32(32 * 16)
            NLF = A.f32(256)
            EB = A.f32(256); CF = A.f32(256); WW = A.f32(256); DEC = A.f32(256); EB2 = A.f32(256)
            v4 = lambda ap: ap.rearrange("p (d c h) -> p d c h", d=2, c=32)
            m1 = A.mark()
            xts = [A.f32(4096).rearrange("p (c t) -> p c t", c=8) for _ in range(2)]
            sq = A.bf(4096).rearrange("p (c t) -> p c t", c=8)
            rs = A.f32(512)
            wg = A.bf(8 * 16).rearrange("p (k n) -> p k n", k=8)
            s.dma(wg, ABrest[j][:, :, 512:528], r=["ABrest%d" % j], w=["wg"])
            g = P("abn%d" % j)
            psg = Rot(PS[1:3])
            Gv = G.rearrange("p (c n) -> p c n", n=16)
            for i in range(NT):
                b = i % 2
                norm_tile(X, i, g, xts[b], "xt%d" % b, HN[:, :, i * TT:(i + 1) * TT], "HN%d" % i, sq, rs)
                ps, pt = psg.next()
                for tc in range(4):
                    ch = i * 4 + tc
                    for kc in range(8):
                        s.mm(ps[:, tc * 16:(tc + 1) * 16], HN[:, kc, ch * 128:(ch + 1) * 128], wg[:, kc, :],
                             start=(kc == 0), stop=(kc == 7), r=["HN%d" % i, "wg"], w=[pt])
                s.tt("dve", G[:, i * 64:(i + 1) * 64], ps[:, 0:64], P("gb%d" % j), ALU.add, r=[pt, "prm"], w=["G"])
            if DEBUG.get("ab_stop") == 0.5:
                A.release(m0)
                s.barrier()
                return False
            G4 = G.rearrange("p (c g h) -> p c g h", g=4, h=4)
            NLF4 = v4(NLF)
            tmpg = A.f32(256)
            tmp4 = v4(tmpg)
            for d in range(2):
                s.act(tmp4[:, d], G4[:, :, 2 * d + 1, :], AF.Exp, scale=-1.0, r=["G"], w=["tmpg"])
            s.act(NLF, tmpg, AF.Ln, bias=1.0, r=["tmpg"], w=["NLF"])
            if DEBUG.get("ab_stop") == 0.6:
                A.release(m0)
                s.barrier()
                return False
            pc_, pct = PS[1]
            s.mm(pc_[:, 0:128], C("maskF"), NLF[:, 0:128], r=["cst", "NLF"], w=[pct])
            s.mm(pc_[:, 128:256], C("maskB"), NLF[:, 128:256], r=["cst", "NLF"], w=[pct])
            pg_, pgt = PS[2]
            s.mm(pg_[:, 0:256], C("ones"), NLF, r=["cst", "NLF"], w=[pgt])
            if DEBUG.get("ab_stop") == 0.7:
                s.cp("dve", EB, pc_[:, 0:256], r=[pct], w=["EB"])
                s.cp("dve", DEC, pg_[:, 0:256], r=[pgt], w=["DEC"])
                A.release(m0)
                s.barrier()
                return False
            s.act(EB, pc_[:, 0:256], AF.Exp, scale=-1.0, r=[pct], w=["EB"])
            s.act(DEC, pg_[:, 0:256], AF.Exp, scale=-1.0, r=[pgt], w=["DEC"])
            tl = A.f32(256)
            tl4 = v4(tl)
            pc4 = v4(pc_[:, 0:256])
            LI = A.f32(256)
            LI4 = v4(LI)
            for d in range(2):
                s.cp("act", LI4[:, d], G4[:, :, 2 * d, :], r=["G"], w=["LI"])
            s.tt("dve", tl, pc_[:, 0:256], LI, ALU.add, r=[pct, "LI"], w=["tl"])
            s.act(CF, tl, AF.Exp, r=["tl"], w=["CF"])
            tl2 = A.f32(256)
            s.tt("dve", tl2, tl, pg_[:, 0:256], ALU.subtract, r=["tl", pgt], w=["tl2"])
            s.act(WW, tl2, AF.Exp, r=["tl2"], w=["WW"])
            EB4, CF4, WW4, DEC4 = v4(EB), v4(CF), v4(WW), v4(DEC)
            EB2v = EB2.rearrange("p (h c d) -> p c h d", c=32, h=4)
            for d in range(2):
                s.cp("act", EB2v[:, :, :, d], EB4[:, d], r=["EB"], w=["EB2"])
            A.release(m1)
            s.barrier()
            if DEBUG.get("ab_stop") == 1:
                A.release(m0)
                return False
            m2 = A.mark()
            raw = A.f32(S + 8)
            acc = A.f32(S)
            QT = A.bf(S); KT = A.bf(S)
            VE = A.bf(32 * 130).rearrange("p (c n) -> p c n", n=130)
            KK = A.bf(32 * 128).rearrange("p (c n) -> p c n", n=128)
            CST = A.bf(33 * 130).rearrange("p (c n) -> p c n", n=130)
            wq = A.bf(8 * 3 * 128).rearrange("p (k t n) -> p k t n", k=8, t=3)
            C32 = [A.f32(130) for _ in range(2)]
            C32b = [A.f32(130) for _ in range(2)]
            vws = [A.bf(130) for _ in range(4)]
            Sf = [A.bf(128) for _ in range(4)]
            Sb = [A.bf(128) for _ in range(4)]
            sm = [A.f32(32) for _ in range(5)]
            tmpo = [A.f32(128) for _ in range(2)]
            hs = [acc[:, 2080 + i_ * 128:2080 + (i_ + 1) * 128] for i_ in range(6)]
            sqj = [acc[:, 2080 + 768 + i_ * 128:2080 + 768 + (i_ + 1) * 128] for i_ in range(2)]
            VF = [acc[:, 3104 + i_ * 72:3104 + i_ * 72 + 65].bitcast(BF16) for i_ in range(4)]
            VB = [acc[:, 3104 + 288 + i_ * 72:3104 + 288 + i_ * 72 + 65].bitcast(BF16) for i_ in range(4)]
            S32 = [A.f32(256) for _ in range(2)]
            hnm = [A.bf(128) for _ in range(2)]
            s.memset("pool", raw, 0.0, w=["raw"])
            s.memset("pool", VE[:, :, 128:130], 1.0, w=["VE"])
            s.memset("pool", CST[:, 0, :], 0.0, w=["CST0"])
            cw = P("conv%d" % j).rearrange("p (c t) -> p c t", t=5)
            psP = Rot(PS[0:4])
            for h in range(DEBUG.get("ab_heads", 4)):
                s.dma(wq, ABqkv[j][h], r=["ABqkv%d_%d_%d" % (j, h, t) for t in range(3)], w=["wq"])
                for t in range(2):
                    for i in range(NT):
                        ps, pt = psP.next()
                        for kc in range(8):
                            s.mm(ps, wq[:, kc, t, :], HN[:, kc, i * TT:(i + 1) * TT], start=(kc == 0), stop=(kc == 7),
                                 r=["wq", "HN%d" % i], w=[pt])
                        s.cp("act" if i % 2 else "dve", raw[:, 2 + i * TT:2 + (i + 1) * TT], ps, r=[pt, "raw"], w=["raw%d" % i])
                    rawt = ["raw%d" % i for i in range(NT)] + ["raw"]
                    if t == 0:
                        VT = KK.rearrange("p c n -> p (c n)")
                        for i in range(NT):
                            ps, pt = psP.next()
                            for kc in range(8):
                                s.mm(ps, wq[:, kc, 2, :], HN[:, kc, i * TT:(i + 1) * TT], start=(kc == 0), stop=(kc == 7),
                                     r=["wq", "HN%d" % i], w=[pt])
                            s.cp("act", VT[:, i * TT:(i + 1) * TT], ps, r=[pt], w=["KK%d" % i])
                        for c4 in range(8):
                            ps, pt = psP.next()
                            psb = ps.bitcast(BF16)
                            for cc in range(4):
                                ch = c4 * 4 + cc
                                s.tr(psb[:, cc * 128:(cc + 1) * 128], VT[:, ch * 128:(ch + 1) * 128], ident_bf, r=["KK%d" % c4, "cstb"], w=[pt])
                            s.cp("dve", VE[:, c4 * 4:(c4 + 1) * 4, 0:128], psb[:, 0:512].rearrange("p (a n) -> p a n", a=4),
                                 r=[pt, "VE"], w=["VE%d" % c4])
                    cc = t * 4 + h
                    s.act(acc, raw[:, 0:S], AF.Copy, scale=cw[:, cc, 0:1], r=rawt + ["prm"],
                          w=["acc"] + ["hs%d" % i_ for i_ in range(6)] + ["sqj0", "sqj1"]
                          + ["VF%d" % i_ for i_ in range(4)] + ["VB%d" % i_ for i_ in range(4)])
                    for tap in range(1, 5):
                        s.stt("dve", acc, raw[:, tap:tap + S], cw[:, cc, tap:tap + 1], acc, ALU.mult, ALU.add,
                              r=rawt + ["prm", "acc"], w=["acc"])
                    if t == 0:
                        s.act(acc, acc, AF.Silu, r=["acc"], w=["acc"])
                        s.ts("dve", QT, acc, QSCALE, None, ALU.mult, r=["acc"], w=["QT"])
                    else:
                        s.act(KT, acc, AF.Silu, r=["acc"], w=["KT"])
                for c4 in range(8):
                    ps, pt = psP.next()
                    psb = ps.bitcast(BF16)
                    for cc in range(4):
                        ch = c4 * 4 + cc
                        s.tr(psb[:, cc * 128:(cc + 1) * 128], KT[:, ch * 128:(ch + 1) * 128], ident_bf, r=["KT", "cstb"], w=[pt])
                    s.cp("dve", KK[:, c4 * 4:(c4 + 1) * 4, :], psb[:, 0:512].rearrange("p (a n) -> p a n", a=4), r=[pt], w=["KK%d" % c4])
                CSTb = acc[:, 0:2080].bitcast(BF16).rearrange("p (c n) -> p c n", n=130)
                s.memset("pool", C32[1], 0.0, w=["C32f1"])
                s.memset("pool", C32b[1], 0.0, w=["C32b1"])
                s.memset("pool", CSTb[:, 31, :], 0.0, w=["acc"])
                psU = Rot(PS[4:8])
                LAS = 2
                ust = {}

                def st_front(k):
                    for d in range(2):
                        c = k if d == 0 else 31 - k
                        vi = (k * 2 + d) % 4
                        vw = vws[vi]; vt = "vw%d" % vi
                        s.act(vw, VE[:, c, :], AF.Copy, scale=WW4[:, d, c, h:h + 1], r=["VE%d" % (c // 4), "VE", "WW"], w=[vt])
                        ps, pt = psU.next()
                        s.mm(ps[:, 0:130], KK[:, c, :], vw, r=["KK%d" % (c // 4), vt], w=[pt])
                        ust[(k, d)] = (ps, pt)

                def st_back(k):
                    for d in range(2):
                        c = k if d == 0 else 31 - k
                        Cx = C32 if d == 0 else C32b
                        cn = "C32f" if d == 0 else "C32b"
                        ps, pt = ust.pop((k, d))
                        cur_, prv_ = k % 2, (k - 1) % 2
                        s.stt("dve", Cx[cur_], Cx[prv_], DEC4[:, d, c, h:h + 1], ps[:, 0:130], ALU.mult, ALU.add,
                              r=["%s%d" % (cn, prv_), "DEC", pt], w=["%s%d" % (cn, cur_)])
                        if d == 0:
                            s.cp("pool", CST[:, c + 1, :], Cx[cur_], r=["%s%d" % (cn, cur_)], w=["CST%d" % (c + 1)])
                        elif c >= 1:
                            s.cp("pool", CSTb[:, c - 1, :], Cx[cur_], r=["%s%d" % (cn, cur_)], w=["acc"])

                for k in range(32 + LAS - 1):
                    if k < 32:
                        st_front(k)
                    if k >= LAS - 1:
                        st_back(k - (LAS - 1))
                GP = 2
                NG = 32 // GP
                psS = Rot(PS[0:2]); psH = Rot(PS[2:6]); psT = Rot(PS[6:8])
                hgc = P("hgc%d" % j)
                sA = {}
                sBk = {}

                def stageA(g):
                    ps, pt = psS.next()
                    sA[g] = (ps, pt)
                    for q in range(GP):
                        c = g * GP + q
                        cs = slice(c * 128, (c + 1) * 128)
                        s.mm(ps[:, q * 128:(q + 1) * 128], KT[:, cs], QT[:, cs], r=["KT", "QT"], w=[pt])
                    s32 = S32[g % 2]; s32t = "S32_%d" % (g % 2)
                    s.cp("act", s32, ps[:, 0:GP * 128], r=[pt], w=[s32t])
                    for q in range(GP):
                        c = g * GP + q
                        bi = c % 4
                        s.tt("pool", Sf[bi], s32[:, q * 128:(q + 1) * 128], C("maskF"), ALU.mult, r=[s32t, "cst"], w=["Sf%d" % bi])
                        s.tt("dve", Sb[bi], s32[:, q * 128:(q + 1) * 128], C("maskB"), ALU.mult, r=[s32t, "cst"], w=["Sb%d" % bi])
                        vet = ["VE%d" % (c // 4), "VE", "CF"]
                        s.act(VF[bi], VE[:, c, :], AF.Copy, scale=CF4[:, 0, c, h:h + 1], r=vet, w=["VF%d" % bi])
                        s.act(VB[bi], VE[:, c, :], AF.Copy, scale=CF4[:, 1, c, h:h + 1], r=vet, w=["VB%d" % bi])

                def stageB(g):
                    c0 = g * GP
                    smg = sm[g % 5]
                    smt = "sm%d" % (g % 5)
                    banks = []
                    for q in range(GP):
                        c = c0 + q
                        cs = slice(c * 128, (c + 1) * 128)
                        bi = c % 4
                        ph_, pht = psH.next()
                        banks.append((ph_, pht))
                        vet = ["VE%d" % (c // 4), "VE"]
                        s.mm(ph_[:, 0:130], Sf[bi], VF[bi], start=True, stop=False, r=["Sf%d" % bi, "VF%d" % bi], w=[pht])
                        s.mm(ph_[:, 0:130], QT[:, cs], CST[:, c, :], start=False, stop=True, r=["QT", "CST%d" % c], w=[pht])
                        s.mm(ph_[:, 130:260], Sb[bi], VB[bi], start=True, stop=False, r=["Sb%d" % bi, "VB%d" % bi], w=[pht])
                        s.mm(ph_[:, 130:260], QT[:, cs], CSTb[:, c, :], start=False, stop=True, r=["QT", "acc"], w=[pht])
                    for q in range(GP):
                        ph_, pht = banks[q]
                        den2 = ph_[:, 0:260].rearrange("p (a n) -> p a n", n=130)[:, :, 128]
                        s.cp("dve", smg[:, 2 * q:2 * q + 2], den2, r=[pht], w=[smt])
                    ebg = EB2[:, h * 64 + c0 * 2:h * 64 + (c0 + GP) * 2]
                    n2 = 2 * GP
                    s.tt("dve", smg[:, 4:4 + n2], smg[:, 0:n2], ebg, ALU.mult, r=[smt, "EB2"], w=[smt])
                    s.add("dve", lambda e, o=smg[:, 8:8 + n2], i_=smg[:, 4:4 + n2]: e.reciprocal(o, i_), r=[smt], w=[smt])
                    s.stt("dve", smg[:, 12:12 + n2], smg[:, 8:8 + n2], -1.0, smg[:, 8:8 + n2], ALU.mult, ALU.max, r=[smt], w=[smt])
                    s.stt("dve", smg[:, 16:16 + n2], smg[:, 12:12 + n2], 1.0, ebg, ALU.min, ALU.mult, r=[smt, "EB2"], w=[smt])
                    sBk[g] = banks

                def stageB1b(g):
                    c0 = g * GP
                    smg = sm[g % 5]
                    smt = "sm%d" % (g % 5)
                    banks = sBk.pop(g)
                    for q in range(GP):
                        c = c0 + q
                        bi = c % 4
                        hi_ = c % 6
                        ph_, pht = banks[q]
                        s.act(tmpo[q], ph_[:, 0:128], AF.Copy, scale=smg[:, 16 + 2 * q:17 + 2 * q], r=[pht, smt], w=["tmpo%d" % q])
                        s.stt("dve", hs[hi_], ph_[:, 130:258], smg[:, 17 + 2 * q:18 + 2 * q], tmpo[q], ALU.mult, ALU.add,
                              r=[pht, smt, "tmpo%d" % q], w=["hs%d" % hi_])
                        s.act(sqj[q], hs[hi_], AF.Square, accum_out=smg[:, 20 + q:21 + q], r=["hs%d" % hi_], w=["sqj%d" % q, smt])

                def stageB2(g):
                    c0 = g * GP
                    smg = sm[g % 5]
                    smt = "sm%d" % (g % 5)
                    s.ts("dve", smg[:, 22:22 + GP], smg[:, 20:20 + GP], 1.0 / 128, EPS, ALU.mult, ALU.add, r=[smt], w=[smt])
                    s.act(smg[:, 24:24 + GP], smg[:, 22:22 + GP], AF.Sqrt, r=[smt], w=[smt])
                    s.add("dve", lambda e, o=smg[:, 26:26 + GP], i_=smg[:, 24:24 + GP]: e.reciprocal(o, i_), r=[smt], w=[smt])

                def stageB2b(g):
                    c0 = g * GP
                    smg = sm[g % 5]
                    smt = "sm%d" % (g % 5)
                    ptp, ptt = psT.next()
                    ptb = ptp.bitcast(BF16)
                    for q in range(GP):
                        c = c0 + q
                        bi = c % 4
                        hi_ = c % 6
                        s.act(hnm[q], hs[hi_], AF.Copy, scale=smg[:, 26 + q:27 + q], r=["hs%d" % hi_, smt], w=["hnm%d" % q])
                        s.tr(ptb[:, q * 128:(q + 1) * 128], hnm[q], ident_bf, r=["hnm%d" % q, "cstb"], w=[ptt])
                    s.act(HAT[:, h, c0 * 128:(c0 + GP) * 128], ptb[:, 0:GP * 128], AF.Copy, scale=hgc[:, h:h + 1],
                          r=[ptt, "prm"], w=["HAT"])

                for g_ in range(NG + 4):
                    if g_ < NG:
                        stageA(g_)
                    if 1 <= g_ <= NG:
                        stageB(g_ - 1)
                    if 2 <= g_ <= NG + 1:
                        stageB1b(g_ - 2)
                    if 3 <= g_ <= NG + 2:
                        stageB2(g_ - 3)
                    if g_ >= 4:
                        stageB2b(g_ - 4)
            A.release(mH)
            s.barrier()
            if DEBUG.get("ab_stop") == 2:
                A.release(m0)
                return False
            m3 = A.mark()
            wr = A.bf(8 * 1552).rearrange("p (k n) -> p k n", k=8)
            wo = A.bf(8 * 1024).rearrange("p (k n) -> p k n", k=8)
            wsb = A.bf(512).rearrange("p (g t) -> p g t", g=4)
            s.dma(wr, ABrest[j], r=["ABrest%d" % j], w=["wr"])
            s.dma(wo, ABout[j], r=["ABout%d" % j], w=["wo"])
            s.cp("dve", wsb, P("wsT%d" % j).rearrange("p (g t) -> p g t", g=4), r=["prm"], w=["wsb"])
            xts = [A.f32(4096).rearrange("p (c t) -> p c t", c=8)] * 2
            HAg = [A.bf(2048).rearrange("p (c t) -> p c t", c=4)] * 2
            HB = [A.bf(2048).rearrange("p (c t) -> p c t", c=4)] * 2
            U = [A.f32(2048).rearrange("p (c t) -> p c t", c=4)] * 2
            VBN = [A.bf(2048).rearrange("p (c t) -> p c t", c=4)] * 2
            gtmp = None
            sg = [A.f32(512) for _ in range(2)]
            vg_ = [A.f32(512) for _ in range(4)]
            t1 = [A.f32(512)] * 2
            s4 = [A.f32(64)]
            bsr = P("bs%d" % j).rearrange("p (g t) -> p g t", g=4)
            vgc = P("vg%d" % j)
            psA = Rot(PS[0:5]); psO = Rot(PS[5:8])
            n = 0
            for i in range(NT):
                b = 0
                ts_ = slice(i * TT, (i + 1) * TT)
                hnt = "HN%d" % i
                s.dma(xts[b], Xtile(X, i), r=["X%d" % i], w=["xt%d" % b])
                s4_ = s4[0]; s4t = "s40"
                for tc in range(4):
                    ch = i * 4 + tc
                    ps, pt = psA.next()
                    for kc in range(8):
                        s.mm(ps, HN[:, kc, ch * 128:(ch + 1) * 128], wr[:, kc, 1040:1552], start=(kc == 0), stop=(kc == 7),
                             r=["wr", hnt], w=[pt])
                    gelu(vg_[tc], ps, pt, gtmp, "vg%d" % tc)
                    for gg in range(4):
                        s.act(t1[0][:, 0:128], vg_[tc][:, gg * 128:(gg + 1) * 128], AF.Square, accum_out=s4_[:, tc * 4 + gg:tc * 4 + gg + 1],
                              r=["vg%d" % tc], w=["t10", s4t])
                s.ts("dve", s4_[:, 16:32], s4_[:, 0:16], 1.0 / 128, EPS, ALU.mult, ALU.add, r=[s4t], w=[s4t])
                s.act(s4_[:, 32:48], s4_[:, 16:32], AF.Sqrt, r=[s4t], w=[s4t])
                s.add("dve", lambda e, o=s4_[:, 48:64], i_=s4_[:, 32:48]: e.reciprocal(o, i_), r=[s4t], w=[s4t])
                for tc in range(4):
                    for gg in range(4):
                        s.ts("dve", VBN[b][:, tc, gg * 128:(gg + 1) * 128], vg_[tc][:, gg * 128:(gg + 1) * 128],
                             s4_[:, 48 + tc * 4 + gg:49 + tc * 4 + gg], None, ALU.mult, r=["vg%d" % tc, s4t], w=["VBN%d_%d" % (b, tc)])
                for mm_ in range(4):
                    ps, pt = psA.next()
                    for kc in range(8):
                        s.mm(ps, wr[:, kc, mm_ * 128:(mm_ + 1) * 128], HN[:, kc, ts_], start=(kc == 0), stop=(kc == 7),
                             r=["wr", hnt], w=[pt])
                    sb_ = n % 2; n += 1
                    s.act(sg[sb_], ps, AF.Sigmoid, r=[pt], w=["sg%d" % sb_])
                    s.tt("dve", HAg[b][:, mm_, :], HAT[:, mm_, ts_], sg[sb_], ALU.mult, r=["HAT", "sg%d" % sb_], w=["HAg%d" % b])
                for mm_ in range(4):
                    ps, pt = psA.next()
                    for kc in range(8):
                        s.mm(ps, wr[:, kc, 528 + mm_ * 128:528 + (mm_ + 1) * 128], HN[:, kc, ts_], start=(kc == 0), stop=(kc == 7),
                             r=["wr", hnt], w=[pt])
                    gelu(U[b][:, mm_, :], ps, pt, gtmp, "U%d_%d" % (b, mm_))
                for gg in range(4):
                    ps, pt = psA.next()
                    for tc in range(4):
                        s.mm(ps[:, tc * 128:(tc + 1) * 128], VBN[b][:, tc, gg * 128:(gg + 1) * 128], wsb[:, gg, :],
                             r=["VBN%d_%d" % (b, tc), "wsb"], w=[pt])
                    tb = 0
                    for tc in range(4):
                        s.stt("dve", t1[tb][:, tc * 128:(tc + 1) * 128], ps[:, tc * 128:(tc + 1) * 128], vgc[:, gg:gg + 1], bsr[:, gg, :],
                              ALU.mult, ALU.add, r=[pt, "prm"], w=["t1%d" % tb])
                    s.tt("dve", HB[b][:, gg, :], t1[tb], U[b][:, gg, :], ALU.mult, r=["t1%d" % tb, "U%d_%d" % (b, gg)], w=["HB%d" % b])
                for jj in range(8):
                    ps, pt = psO.next()
                    for kc in range(8):
                        rhs = HAg[b][:, kc, :] if kc < 4 else HB[b][:, kc - 4, :]
                        s.mm(ps, wo[:, kc, jj * 128:(jj + 1) * 128], rhs, start=(kc == 0), stop=(kc == 7),
                             r=["wo", "HAg%d" % b, "HB%d" % b], w=[pt])
                    s.tt("dve", xts[b][:, jj, :], ps, xts[b][:, jj, :], ALU.add, r=[pt, "xt%d" % b], w=["xo%d_%d" % (b, jj)])
                s.dma(Xtile(Y, i), xts[b], r=["xo%d_%d" % (b, jj) for jj in range(8)] + ["xt%d" % b], w=["X%d" % i, "xt%d" % b])
            A.release(m3)
            A.release(m0)
            s.barrier()

        def blk_mla(j, X, Y):
            m0 = A.mark()
            CQN = A.bf(3 * S).rearrange("p (c t) -> p c t", c=3)
            CKVN = A.bf(2 * S).rearrange("p (c t) -> p c t", c=2)
            KE = [A.bf(S) for _ in range(2)]
            TQ = A.f32(S)
            WQX = A.bf(3 * 16 * 128).rearrange("p (k h n) -> p k h n", k=3, h=16)
            WUKV = A.bf(2 * 2048).rearrange("p (k n) -> p k n", k=2)
            m1 = A.mark()
            TK = A.f32(S)
            wcin = A.bf(8 * 704).rearrange("p (k n) -> p k n", k=8)
            mt = A.mark()
            wuq = A.bf(3 * 1536).rearrange("p (k n) -> p k n", k=3)
            s.dma(wcin[:, :, 0:672], Ccin[j], r=["Ccin%d" % j], w=["wcin"])
            s.ts("dve", wcin[:, :, 672:688], wcin[:, :, 656:672], -1.0, None, ALU.mult, r=["wcin"], w=["wcin"])
            s.cp("dve", wcin[:, :, 688:704], wcin[:, :, 640:656], r=["wcin"], w=["wcin"])
            s.dma(wuq, Cuq[j], r=["Cuq%d" % j], w=["wuq"])
            wuq4 = wuq.rearrange("p k (h n) -> p k h n", h=16)
            for kc in range(3):
                s.cp("act", WQX[:, kc, :, 0:96], wuq4[:, kc], r=["wuq"], w=["WQX"])
                s.ts("dve", WQX[:, kc, :, 96:112], wuq4[:, kc, :, 80:96], -1.0, None, ALU.mult, r=["wuq"], w=["WQX"])
                s.cp("dve", WQX[:, kc, :, 112:128], wuq4[:, kc, :, 64:80], r=["wuq"], w=["WQX"])
            s.dma(WUKV, Cukv[j], r=["Cukv%d" % j], w=["WUKV"])
            pi = A.i32(S)
            pf = A.f32(S)
            rr = A.f32(S)
            s.dma(pi, posb, w=["pi"])
            s.cp("dve", pf, pi, r=["pi"], w=["pf"])
            for (tab, np_, fcol, pcol, tname) in ((TQ, 128, C("fq"), C("ph"), "TQ"), (TK, 64, C("fk"), C("pk"), "TK")):
                s.ts("dve", rr[0:np_], pf[0:np_], fcol[0:np_], pcol[0:np_], ALU.mult, ALU.add, r=["pf", "cst", "pi"], w=["rr"])
                s.cp("dve", pi[0:np_], rr[0:np_], r=["rr"], w=["pi"])
                s.cp("dve", tab[0:np_], pi[0:np_], r=["pi"], w=[tname])
                s.tt("dve", rr[0:np_], rr[0:np_], tab[0:np_], ALU.subtract, r=["rr", tname], w=["rr"])
                s.act(tab[0:np_], rr[0:np_], AF.Sin, scale=TWO_PI, r=["rr"], w=[tname])
            s.ts("dve", TQ, TQ, ATT_SCALE, None, ALU.mult, r=["TQ"], w=["TQ"])
            A.release(mt)
            s.barrier()
            if DEBUG.get("mla_stop") == 0:
                A.release(m0)
                return False
            xts = [A.f32(4096).rearrange("p (c t) -> p c t", c=8)] * 2
            hns = [A.bf(4096).rearrange("p (c t) -> p c t", c=8) for _ in range(2)]
            sq = A.bf(4096).rearrange("p (c t) -> p c t", c=8)
            rs = A.f32(512)
            CQ = A.f32(5 * 512).rearrange("p (c t) -> p c t", c=5)
            sq2 = A.bf(5 * 512).rearrange("p (c t) -> p c t", c=5)
            rs2 = [A.f32(512) for _ in range(2)]
            tk = A.f32(512)
            g = P("cn%d" % j)
            s.memset("pool", tk, 0.0, w=["tk0"])
            psA = Rot(PS[1:6])
            for i in range(NT):
                b = i % 2
                ts_ = slice(i * TT, (i + 1) * TT)
                hn = hns[b]
                if i == 0:
                    norm_tile(X, 0, g, xts[0], "xt0", hns[0], "hn0", sq, rs)
                for mm_ in range(5):
                    ps, pt = psA.next()
                    for kc in range(8):
                        s.mm(ps, wcin[:, kc, mm_ * 128:(mm_ + 1) * 128], hn[:, kc, :], start=(kc == 0), stop=(kc == 7),
                             r=["wcin", "hn%d" % b], w=[pt])
                    s.act(sq2[:, mm_, :], ps, AF.Square, r=[pt], w=["sq2_%d" % mm_])
                    s.cp("dve", CQ[:, mm_, :], ps, r=[pt], w=["CQ%d" % mm_])
                for (lo, hi, dim, gcol, dst, dn, ri) in ((0, 3, 384.0, P("qg%d" % j), CQN, "CQN", 0), (3, 5, 256.0, P("kvg%d" % j), CKVN, "CKVN", 1)):
                    ps, pt = psA.next()
                    for mm_ in range(lo, hi):
                        s.mm(ps, ones_bf, sq2[:, mm_, :], start=(mm_ == lo), stop=(mm_ == hi - 1), r=["sq2_%d" % mm_, "cstb"], w=[pt])
                    rt = "rs2%d" % ri
                    s.ts("dve", rs2[ri], ps, 1.0 / dim, EPS, ALU.mult, ALU.add, r=[pt], w=[rt])
                    s.act(rs2[ri], rs2[ri], AF.Ln, r=[rt], w=[rt])
                    s.act(rs2[ri], rs2[ri], AF.Exp, scale=-0.5, r=[rt], w=[rt])
                    for mm_ in range(lo, hi):
                        s.stt("dve", dst[:, mm_ - lo, ts_], CQ[:, mm_, :], gcol[:, mm_ - lo:mm_ - lo + 1], rs2[ri], ALU.mult, ALU.mult,
                              r=["CQ%d" % mm_, rt, "prm"], w=["%s%d" % (dn, i)])
                ps, pt = psA.next()
                for kc in range(8):
                    s.mm(ps[0:64, :], wcin[:, kc, 640:704], hn[:, kc, :], start=(kc == 0), stop=(kc == 7), r=["wcin", "hn%d" % b], w=[pt])
                if i + 1 < NT:
                    norm_tile(X, i + 1, g, xts[0], "xt0", hns[1 - b], "hn%d" % (1 - b), sq, rs)
                s.tt("dve", tk[0:64], ps[0:64, :], TK[0:64, ts_], ALU.mult, r=[pt, "TK", "tk0"], w=["tk"])
                ps2, pt2 = psA.next()
                s.mm(ps2, C("sel"), tk, r=["cst", "tk", "tk0"], w=[pt2])
                s.cp("act", KE[0][64:128, ts_], ps2[64:128, :], r=[pt2], w=["KEr0"])
                s.cp("dve", KE[1][64:128, ts_], ps2[64:128, :], r=[pt2], w=["KEr1"])
            A.release(m1)
            s.barrier()
            if DEBUG.get("mla_stop") == 1:
                A.release(m0)
                return False
            m2 = A.mark()
            OT = A.bf(8 * S).rearrange("p (c t) -> p c t", c=8)
            m2b = A.mark()
            VX = [A.bf(32 * 128).rearrange("p (c n) -> p c n", n=128) for _ in range(2)]
            QX = [A.bf(512) for _ in range(2)]
            PT = [A.bf(512) for _ in range(4)]
            rden = [A.f32(512) for _ in range(2)]
            for b in range(2):
                s.memset("pool", VX[b][:, :, 64:128], 1.0, w=["VXo%d" % b])
            psS = Rot(PS[0:3]); psO = Rot(PS[3:5]); psQ = Rot(PS[5:6]); psK = Rot(PS[6:8])
            NH = DEBUG.get('mla_heads', 16)
            LA = 2

            def kv_prep(h):
                hb = h % 2
                for i in range(NT):
                    ts_ = slice(i * TT, (i + 1) * TT)
                    ps, pt = psK.next()
                    for kc in range(2):
                        s.mm(ps[0:64, :], WUKV[:, kc, h * 128:h * 128 + 64], CKVN[:, kc, ts_], start=(kc == 0), stop=(kc == 1),
                             r=["WUKV", "CKVN%d" % i], w=[pt])
                    s.cp("dve", KE[hb][0:64, ts_], ps[0:64, :], r=[pt], w=["KE%d_%d" % (hb, i)])
                for c8 in range(4):
                    ps, pt = psK.next()
                    for cc in range(8):
                        ch = c8 * 8 + cc
                        for kc in range(2):
                            s.mm(ps[:, cc * 64:(cc + 1) * 64], CKVN[:, kc, ch * 128:(ch + 1) * 128], WUKV[:, kc, h * 128 + 64:h * 128 + 128],
                                 start=(kc == 0), stop=(kc == 1), r=["WUKV", "CKVN%d" % (ch // 4)], w=[pt])
                    s.cp("dve", VX[hb][:, c8 * 8:(c8 + 1) * 8, 0:64], ps.rearrange("p (a n) -> p a n", n=64),
                         r=[pt, "VXo%d" % hb], w=["VX%d_%d" % (hb, c8)])

            def q_prep(h, qt):
                ts_ = slice(qt * TT, (qt + 1) * TT)
                pq, pqt = psQ.next()
                for kc in range(3):
                    s.mm(pq, WQX[:, kc, h, :], CQN[:, kc, ts_], start=(kc == 0), stop=(kc == 2), r=["WQX", "CQN%d" % qt], w=[pqt])
                qb = qt % 2
                s.tt("dve", QX[qb], pq, TQ[:, ts_], ALU.mult, r=[pqt, "TQ"], w=["QX%d" % qb])

            npt = 0
            kv_prep(0)
            for h in range(NH):
                hb = h % 2
                ket = ["KE%d_%d" % (hb, i) for i in range(NT)] + ["KEr%d" % hb]
                vxt = ["VX%d_%d" % (hb, c8) for c8 in range(4)] + ["VXo%d" % hb]
                seq = [(qt, kc) for qt in range(NT) for kc in range(32)]
                q_prep(h, 0)
                pos_ = {}
                ptb = {}
                for n in range(len(seq) + LA):
                    if n < len(seq):
                        qt, kc = seq[n]
                        if kc == 0:
                            pos_[qt] = psO.next()
                        if kc == 12 and qt + 1 < NT:
                            q_prep(h, qt + 1)
                        if qt == 4 and kc == 20 and h + 1 < NH:
                            kv_prep(h + 1)
                        qb = qt % 2
                        ps, pt = psS.next()
                        s.mm(ps, KE[hb][:, kc * 128:(kc + 1) * 128], QX[qb], r=ket + ["QX%d" % qb], w=[pt])
                        pb = npt % 4; npt += 1
                        ptb[n] = pb
                        s.act(PT[pb], ps, AF.Exp, r=[pt], w=["PT%d" % pb])
                    if n >= LA:
                        qt2, kc2 = seq[n - LA]
                        po, pot = pos_[qt2]
                        pb = ptb.pop(n - LA)
                        s.mm(po, VX[hb][:, kc2, :], PT[pb], start=(kc2 == 0), stop=(kc2 == 31), r=vxt + ["PT%d" % pb], w=[pot])
                        if kc2 == 31:
                            ts2 = slice(qt2 * TT, (qt2 + 1) * TT)
                            rb = qt2 % 2
                            s.add("dve", lambda e, o=rden[rb][64:128], i_=po[64:128, :]: e.reciprocal(o, i_), r=[pot], w=["rden%d" % rb])
                            r0 = (h % 2) * 64
                            s.tt("dve", OT[r0:r0 + 64, h // 2, ts2], po[0:64, :], rden[rb][64:128], ALU.mult,
                                 r=[pot, "rden%d" % rb], w=["OT%d" % qt2])
            A.release(m2b)
            s.barrier()
            A.off = m0
            wo = A.bf(8 * 1024).rearrange("p (k n) -> p k n", k=8)
            xts = [A.f32(4096).rearrange("p (c t) -> p c t", c=8) for _ in range(2)]
            s.dma(wo, Cout[j], r=["Cout%d" % j], w=["wo"])
            psO = Rot(PS)
            load_x(X, 0, xts[0], "xt0")
            for i in range(NT):
                b = i % 2
                ts_ = slice(i * TT, (i + 1) * TT)
                if i + 1 < NT:
                    load_x(X, i + 1, xts[1 - b], "xt%d" % (1 - b))
                for jj in range(8):
                    ps, pt = psO.next()
                    for kc in range(8):
                        s.mm(ps, wo[:, kc, jj * 128:(jj + 1) * 128], OT[:, kc, ts_], start=(kc == 0), stop=(kc == 7), r=["wo", "OT%d" % i], w=[pt])
                    s.tt("dve", xts[b][:, jj, :], ps, xts[b][:, jj, :], ALU.add, r=[pt, "xt%d" % b], w=["xo%d_%d" % (b, jj)])
                s.dma(Xtile(Y, i), xts[b], r=["xo%d_%d" % (b, jj) for jj in range(8)] + ["xt%d" % b], w=["X%d" % i, "xt%d" % b])
            A.release(m0)
            s.barrier()

        cur = 0
        body = [b for b in blocks if b not in ("load", "final")]
        if body:
            conv_for(body[0])
        Xcur = x_in
        for bi, b in enumerate(body):
            if bi + 1 < len(body):
                conv_for(body[bi + 1])
            X, Y = Xcur, XS[cur]
            if b.startswith("ffn"):
                blk_ffn(int(b[3:]), X, Y)
            elif b.startswith("ab"):
                if blk_ab(int(b[2:]), X, Y) is False:
                    continue
            elif b.startswith("mla"):
                if blk_mla(int(b[3:]), X, Y) is False:
                    continue
            Xcur = Y
            cur = 1 - cur
        blk_final(Xcur)
        s.emit(nc)
        print("ops:", s.nops, "arena peak words:", A.peak)
    return nc


_W_NAMES = ["ab_w_in", "ab_w_out", "c_w_in", "c_w_uq", "c_w_ukv", "c_w_out", "ffn_w1", "ffn_w2"]


def run(inputs, blocks=None, n_cores=8, trace=False):
    nc = build(blocks)
    need = needed_weights(ALL_BLOCKS if blocks is None else blocks)
    prm, cst = host_pack(inputs)
    x = np.asarray(inputs["x"], np.float32)
    xT = np.ascontiguousarray(x.transpose(0, 2, 1)).reshape(x.shape[0], 8, 128, S)
    pos = np.asarray(inputs["positions"], np.int32)
    shared = {k: np.ascontiguousarray(np.asarray(inputs[k], np.float32)) for k in _W_NAMES if k in need}
    in_maps = []
    for c in range(n_cores):
        mp = {"xT": xT[c], "posb": np.ascontiguousarray(np.broadcast_to(pos[c][None, :], (128, S))),
              "prm": prm, "cst": cst}
        mp.update(shared)
        in_maps.append(mp)
    res = run_bass_kernel_spmd(nc, in_maps, core_ids=list(range(n_cores)), trace=trace)
    out = np.stack([np.asarray(r["outT"]).reshape(D, S).T for r in res.results], axis=0)
    return out, res


def kernel(**inputs):
    import os, json
    if os.environ.get("KBLOCKS") is not None:
        DEBUG.update(json.loads(os.environ.get("KDEBUG", "{}")))
        out, _ = run(inputs, blocks=["load"] + [b for b in os.environ["KBLOCKS"].split(",") if b] + ["final"])
        return out.astype(np.float32)
    out, _ = run(inputs)
    return out.astype(np.float32)
```
